# Optimizing a Trainium2 kernel written in Bass

```python
import math
import jax, jax.numpy as jnp
from jax import lax
import numpy as np

D_MODEL = 1024
BATCH = 8
SEQ = 2048
DEPTH = 1

D_MIX = D_MODEL
LRU_WIDTH = D_MIX // 2
LRU_BLOCKS = 8
LRU_BLOCK = LRU_WIDTH // LRU_BLOCKS
CONV_WIDTH = 4
LRU_C = 8.0
N_HEADS = 8
N_KV_HEADS = 2
GROUP = N_HEADS // N_KV_HEADS
HEAD_DIM = 64
ATTN_WIDTH = N_HEADS * HEAD_DIM
KV_WIDTH = N_KV_HEADS * HEAD_DIM
WINDOW = 128
BLOCK = 128
N_BUCKETS = 32
MAX_DISTANCE = 128
PEER_HEADS = 8
N_KEYS = 128
N_EXPERTS = N_KEYS * N_KEYS
D_QUERY = 256
D_HALF = D_QUERY // 2
TOPK = 16
PEER_CHUNK = 128
IN_COLS = 2 * LRU_WIDTH + ATTN_WIDTH + 2 * KV_WIDTH
EPS = 1e-6
NEG_INF = -1e30
SCALE = HEAD_DIM ** -0.5

kernel_name = "hymba_rglru_swa_sink_peer"


def rmsnorm(x, g):
    xf = x.astype(jnp.float32)
    y = xf * lax.rsqrt(jnp.mean(xf * xf, axis=-1, keepdims=True) + EPS)
    return (y * g.astype(jnp.float32)).astype(x.dtype)


def t5_bucket(rel):
    n = jnp.maximum(rel, 0)
    max_exact = N_BUCKETS // 2
    nf = jnp.maximum(n, 1).astype(jnp.float32)
    large = max_exact + (jnp.log(nf / max_exact) / math.log(MAX_DISTANCE / max_exact)
                         * (N_BUCKETS - max_exact)).astype(jnp.int32)
    large = jnp.minimum(large, N_BUCKETS - 1)
    return jnp.where(n < max_exact, n, large)


def band_bias_and_mask(rel_bias, S):
    nb = S // BLOCK
    i = jnp.arange(BLOCK)[:, None]
    j = jnp.arange(2 * BLOCK)[None, :]
    rel = BLOCK + i - j
    bias = rel_bias.astype(jnp.float32)[t5_bucket(rel)]
    bias = jnp.transpose(bias, (2, 0, 1)).reshape(N_KV_HEADS, GROUP, BLOCK, 2 * BLOCK)
    kpos = jnp.arange(nb)[:, None, None] * BLOCK - BLOCK + j[None]
    mask = (rel >= 0)[None] & (rel < WINDOW)[None] & (kpos >= 0)
    return bias, mask


def rglru_mixer(xb, gb, conv_w, conv_b, w_gate_a, b_gate_a, w_gate_x, b_gate_x, lru_L):
    B, S, _ = xb.shape
    xp = jnp.pad(xb, ((0, 0), (CONV_WIDTH - 1, 0), (0, 0)))
    xc = conv_b
    for tap in range(CONV_WIDTH):
        xc = xc + xp[:, tap:tap + S] * conv_w[tap]
    xblk = xc.reshape(B, S, LRU_BLOCKS, LRU_BLOCK)
    r = jax.nn.sigmoid((jnp.einsum('bsni,nij->bsnj', xblk, w_gate_a) + b_gate_a)
                       .astype(jnp.float32)).reshape(B, S, LRU_WIDTH)
    ig = jax.nn.sigmoid((jnp.einsum('bsni,nij->bsnj', xblk, w_gate_x) + b_gate_x)
                        .astype(jnp.float32)).reshape(B, S, LRU_WIDTH)
    log_a = -LRU_C * r * jax.nn.softplus(-lru_L.astype(jnp.float32))
    a = jnp.exp(log_a)
    b = jnp.sqrt(-jnp.expm1(2.0 * log_a)) * (ig * xc.astype(jnp.float32))

    def combine(c1, c2):
        a1, b1 = c1
        a2, b2 = c2
        return a1 * a2, a2 * b1 + b2

    _, h = lax.associative_scan(combine, (a, b), axis=1)
    return (h * jax.nn.gelu(gb.astype(jnp.float32))).astype(xb.dtype)


def swa_mixer(q, k, v, q_norm_g, k_norm_g, sinks, pos_bias, mask):
    B, S, _ = q.shape
    nb = S // BLOCK
    qh = rmsnorm(q.reshape(B, S, N_HEADS, HEAD_DIM), q_norm_g).astype(jnp.float32)
    kh = rmsnorm(k.reshape(B, S, N_KV_HEADS, HEAD_DIM), k_norm_g).astype(jnp.float32)
    vh = v.reshape(B, S, N_KV_HEADS, HEAD_DIM).astype(jnp.float32)
    qb = qh.reshape(B, nb, BLOCK, N_KV_HEADS, GROUP, HEAD_DIM)

    def band(t):
        tp = jnp.pad(t, ((0, 0), (BLOCK, 0), (0, 0), (0, 0)))
        tp = tp.reshape(B, nb + 1, BLOCK, N_KV_HEADS, HEAD_DIM)
        return jnp.concatenate([tp[:, :-1], tp[:, 1:]], axis=2)

    kw, vw = band(kh), band(vh)
    s = jnp.einsum('bnqhgd,bnkhd->bnhgqk', qb, kw) * SCALE + pos_bias
    s = jnp.where(mask[None, :, None, None], s, NEG_INF)
    sink = sinks.astype(jnp.float32).reshape(N_KV_HEADS, GROUP)[None, None, :, :, None]
    m = jnp.maximum(jnp.max(s, axis=-1), sink)
    p = jnp.exp(s - m[..., None])
    denom = jnp.sum(p, axis=-1) + jnp.exp(sink - m)
    o = jnp.einsum('bnhgqk,bnkhd->bnqhgd', p / denom[..., None], vw)
    return o.reshape(B, S, ATTN_WIDTH).astype(q.dtype)


def peer(xn, w_query, sub_keys, expert_u, expert_v):
    B, S, D = xn.shape
    T = B * S
    xt = xn.reshape(T, D)
    q = (xt @ w_query).reshape(T, PEER_HEADS, 2, D_HALF).astype(jnp.float32)
    s = jnp.einsum('thcd,hcnd->thcn', q, sub_keys.astype(jnp.float32))
    s_top, i_top = lax.top_k(s, TOPK)
    cand = (s_top[:, :, 0, :, None] + s_top[:, :, 1, None, :]).reshape(T, PEER_HEADS, TOPK * TOPK)
    cand_idx = (i_top[:, :, 0, :, None] * N_KEYS + i_top[:, :, 1, None, :]).reshape(T, PEER_HEADS, TOPK * TOPK)
    best, pos = lax.top_k(cand, TOPK)
    idx = jnp.take_along_axis(cand_idx, pos, axis=-1)
    g = jax.nn.softmax(best, axis=-1)
    nchunks = T // PEER_CHUNK

    def chunk(args):
        xc, ic, gc = args
        u = expert_u[ic].astype(jnp.float32)
        act = jnp.einsum('chkd,cd->chk', u, xc.astype(jnp.float32))
        w = gc * jax.nn.gelu(act)
        v = expert_v[ic].astype(jnp.float32)
        return jnp.einsum('chk,chkd->cd', w, v)

    out = lax.map(chunk, (xt.reshape(nchunks, PEER_CHUNK, D),
                          idx.reshape(nchunks, PEER_CHUNK, PEER_HEADS, TOPK),
                          g.reshape(nchunks, PEER_CHUNK, PEER_HEADS, TOPK)))
    return out.reshape(B, S, D).astype(xn.dtype)


def setup_inputs(seed: int = 0) -> dict:
    key = jax.random.key(seed)
    ks = jax.random.split(key, 24)
    f32 = jnp.float32
    nrm = lambda k, shape, scale: jax.random.normal(k, shape, f32) * scale
    gain = lambda k, shape: 1.0 + 0.05 * jax.random.normal(k, shape, f32)
    u = jax.random.uniform(ks[9], (DEPTH, LRU_WIDTH), f32, minval=0.9, maxval=0.999)
    a0 = u ** (1.0 / LRU_C)
    lru_L = jnp.log(a0) - jnp.log1p(-a0)
    return {
        "x": nrm(ks[0], (BATCH, SEQ, D_MODEL), 1.0),
        "ln_mix_g": gain(ks[1], (DEPTH, D_MODEL)),
        "w_in": nrm(ks[2], (DEPTH, D_MODEL, IN_COLS), D_MODEL ** -0.5),
        "conv_w": nrm(ks[3], (DEPTH, CONV_WIDTH, LRU_WIDTH), CONV_WIDTH ** -0.5),
        "conv_b": nrm(ks[4], (DEPTH, LRU_WIDTH), 0.02),
        "w_gate_a": nrm(ks[5], (DEPTH, LRU_BLOCKS, LRU_BLOCK, LRU_BLOCK), LRU_BLOCK ** -0.5),
        "b_gate_a": nrm(ks[6], (DEPTH, LRU_BLOCKS, LRU_BLOCK), 0.02),
        "w_gate_x": nrm(ks[7], (DEPTH, LRU_BLOCKS, LRU_BLOCK, LRU_BLOCK), LRU_BLOCK ** -0.5),
        "b_gate_x": nrm(ks[8], (DEPTH, LRU_BLOCKS, LRU_BLOCK), 0.02),
        "lru_L": lru_L,
        "q_norm_g": gain(ks[10], (DEPTH, HEAD_DIM)),
        "k_norm_g": gain(ks[11], (DEPTH, HEAD_DIM)),
        "sinks": nrm(ks[12], (DEPTH, N_HEADS), 0.5),
        "lru_out_g": gain(ks[13], (DEPTH, LRU_WIDTH)),
        "attn_out_g": gain(ks[14], (DEPTH, ATTN_WIDTH)),
        "w_out": nrm(ks[15], (DEPTH, D_MIX, D_MODEL), D_MIX ** -0.5),
        "ln_ffn_g": gain(ks[16], (DEPTH, D_MODEL)),
        "w_query": nrm(ks[17], (DEPTH, D_MODEL, PEER_HEADS * D_QUERY), D_MODEL ** -0.5),
        "sub_keys": nrm(ks[18], (DEPTH, PEER_HEADS, 2, N_KEYS, D_HALF), D_HALF ** -0.5),
        "expert_u": nrm(ks[19], (DEPTH, N_EXPERTS, D_MODEL), D_MODEL ** -0.5),
        "expert_v": nrm(ks[20], (DEPTH, N_EXPERTS, D_MODEL), (PEER_HEADS * TOPK) ** -0.5),
        "rel_bias": nrm(ks[21], (N_BUCKETS, N_HEADS), 0.5),
    }


def reference(x, ln_mix_g, w_in, conv_w, conv_b, w_gate_a, b_gate_a, w_gate_x, b_gate_x,
              lru_L, q_norm_g, k_norm_g, sinks, lru_out_g, attn_out_g, w_out, ln_ffn_g,
              w_query, sub_keys, expert_u, expert_v, rel_bias):
    S = x.shape[1]
    pos_bias, mask = band_bias_and_mask(rel_bias, S)
    splits = [LRU_WIDTH, 2 * LRU_WIDTH, 2 * LRU_WIDTH + ATTN_WIDTH,
              2 * LRU_WIDTH + ATTN_WIDTH + KV_WIDTH]
    for l in range(DEPTH):
        h = rmsnorm(x, ln_mix_g[l])
        proj = h @ w_in[l]
        xb, gb, q, k, v = jnp.split(proj, splits, axis=-1)
        y_lru = rglru_mixer(xb, gb, conv_w[l], conv_b[l], w_gate_a[l], b_gate_a[l],
                            w_gate_x[l], b_gate_x[l], lru_L[l])
        y_att = swa_mixer(q, k, v, q_norm_g[l], k_norm_g[l], sinks[l], pos_bias, mask)
        mix = jnp.concatenate([rmsnorm(y_lru, lru_out_g[l]), rmsnorm(y_att, attn_out_g[l])], axis=-1)
        x = x + mix @ w_out[l]
        x = x + peer(rmsnorm(x, ln_ffn_g[l]), w_query[l], sub_keys[l], expert_u[l], expert_v[l])
    return x
```

```python
import os
from contextlib import ExitStack
import numpy as np
import concourse.bass as bass
import concourse.mybir as mybir
from concourse.bass_utils import run_bass_kernel_spmd

F32 = mybir.dt.float32
I32 = mybir.dt.int32
U32 = mybir.dt.uint32
AF = mybir.ActivationFunctionType
ALU = mybir.AluOpType
AX = mybir.AxisListType

S = 2048
DM = 1024
NT = S // 128
EPS = 1e-6
SCALE = 64 ** -0.5
NEXP = 16384
MASKV = -200.0

R_CW, R_CB, R_BA, R_BX, R_L, R_GLRU, R_GATT, R_GMIX, R_QG, R_KG, NROWS = 0, 16, 20, 24, 28, 32, 36, 40, 48, 49, 50


class Trk:
    def __init__(self, nc, es, kq=12):
        self.nc = nc
        self.eng = {"pe": nc.tensor, "act": nc.scalar, "dve": nc.vector, "pool": nc.gpsimd, "sp": nc.sync}
        self.csem = {e: es.enter_context(nc.semaphore("c_" + e)) for e in ("pe", "act", "dve", "pool")}
        self.kq = kq
        self.dsem = {q: [es.enter_context(nc.semaphore("d_%s%d" % (q, i))) for i in range(kq)] for q in ("sp", "pool")}
        self.duse = {q: [0] * kq for q in self.dsem}
        self.dn = {q: 0 for q in self.dsem}
        self.prog = {e: [] for e in self.eng}
        self.cnt = {e: 0 for e in self.csem}
        self.waited = {e: {} for e in self.eng}
        self.last_w = {}
        self.readers = {}
        self.same = {"pe": False, "act": True, "dve": True, "pool": True, "sp": True}

    def _sem(self, k):
        return self.csem[k[1]] if k[0] == "c" else self.dsem[k[1]][k[2]]

    def _deps(self, r, w):
        deps = {}

        def add(k, v):
            if deps.get(k, 0) < v:
                deps[k] = v
        for t in r:
            if t in self.last_w:
                add(*self.last_w[t])
        for t in w:
            if t in self.last_w:
                add(*self.last_w[t])
            for k, v in self.readers.get(t, {}).items():
                add(k, v)
        return deps

    def _commit(self, ev, r, w):
        for t in w:
            self.last_w[t] = ev
            self.readers[t] = {}
        for t in r:
            d = self.readers.setdefault(t, {})
            if d.get(ev[0], 0) < ev[1]:
                d[ev[0]] = ev[1]

    def _waits(self, e, deps, own=None):
        ws = []
        for k, v in deps.items():
            if k == own and not self.same[e]:
                continue
            if self.waited[e].get(k, 0) >= v:
                continue
            self.waited[e][k] = v
            ws.append((self._sem(k), v))
        return ws

    def op(self, e, fn, r=(), w=()):
        psr = [t for t in r if isinstance(t, tuple) and t[0] == "ps"]
        if psr:
            w = list(w) + [t for t in psr if t not in w]
        own = ("c", e)
        deps = self._deps(r, w)
        ws = self._waits(e, deps, own)
        self.cnt[e] += 1
        ev = (own, self.cnt[e])
        sem = self.csem[e]

        def run(eng, ws=ws, fn=fn, sem=sem):
            for s_, v_ in ws:
                eng.wait_ge(s_, v_)
            fn(eng).then_inc(sem, 1)
        self.prog[e].append(run)
        self._commit(ev, r, w)

    def dma(self, q, fn, r=(), w=()):
        slot = self.dn[q] % self.kq
        self.dn[q] += 1
        deps = self._deps(r, w)
        k = ("d", q, slot)
        if self.duse[q][slot] > 0:
            v = 16 * self.duse[q][slot]
            if deps.get(k, 0) < v:
                deps[k] = v
        ws = self._waits(q, deps, None)
        self.duse[q][slot] += 1
        ev = (k, 16 * self.duse[q][slot])
        sem = self.dsem[q][slot]

        def run(eng, ws=ws, fn=fn, sem=sem):
            for s_, v_ in ws:
                eng.wait_ge(s_, v_)
            fn(eng).then_inc(sem, 16)
        self.prog[q].append(run)
        self._commit(ev, r, w)

    def barrier(self):
        evs = {}
        for e, c in self.cnt.items():
            if c:
                evs[("c", e)] = c
        for q in self.dsem:
            for i, u in enumerate(self.duse[q]):
                if u:
                    evs[("d", q, i)] = 16 * u
        for e in self.eng:
            ws = self._waits(e, dict(evs), None)

            def run(eng, ws=ws):
                for s_, v_ in ws:
                    eng.wait_ge(s_, v_)
            self.prog[e].append(run)

    def finish(self):
        evs = {}
        for q in self.dsem:
            for i, u in enumerate(self.duse[q]):
                if u:
                    evs[("d", q, i)] = 16 * u
        ws = self._waits("sp", evs, None)

        def run(eng, ws=ws):
            for s_, v_ in ws:
                eng.wait_ge(s_, v_)
        self.prog["sp"].append(run)

    def emit(self, block):
        for e, dec in (("sp", block.sync), ("act", block.scalar), ("dve", block.vector),
                       ("pool", block.gpsimd), ("pe", block.tensor)):
            def f(engobj, prog=self.prog[e]):
                for c in prog:
                    c(engobj)
            dec(f)


class Arena:
    def __init__(self, t, n):
        self.t, self.n, self.off = t, n, 0

    def reset(self):
        self.off = 0

    def alloc(self, *shape, dt=None):
        n = 1
        for s_ in shape:
            n *= s_
        assert self.off + n <= self.n, ("arena overflow", self.off, n, self.n)
        ap = self.t[:, self.off:self.off + n]
        self.off += n
        if len(shape) == 2:
            ap = ap.rearrange("p (a b) -> p a b", a=shape[0])
        elif len(shape) == 3:
            ap = ap.rearrange("p (a b c) -> p a b c", a=shape[0], b=shape[1])
        if dt is not None:
            ap = ap.bitcast(dt)
        return ap


def build_nc(stage=2, nt=NT):
    nc = bass.Bass("TRN2", target_bir_lowering=False)

    def D(name, shape, dt=F32):
        return nc.dram_tensor(name, shape, dt, kind="ExternalInput").ap()
    x_d = D("x", [S, DM])
    vecs_d = D("vecs", [NROWS, 128])
    w_in_d = D("w_in", [DM, 1792])
    wga_d = D("w_gate_a", [8, 64, 64])
    wgx_d = D("w_gate_x", [8, 64, 64])
    sinks_d = D("sinks", [1, 8])
    w_out_d = D("w_out", [DM, DM])
    gffn_d = D("ln_ffn_g", [1, DM])
    wq_d = D("w_query", [DM, 2048])
    skT_d = D("skT", [16, 128, 128])
    eu_d = D("expert_u", [NEXP, DM])
    ev_d = D("expert_v", [NEXP, DM])
    bias_d = D("bias_band", [128, 2048])
    mask_d = D("mask_band", [128, 2048])
    out_d = nc.dram_tensor("out", [S, DM], F32, kind="ExternalOutput").ap()

    with ExitStack() as es:
        def sb(name, shape, dt=F32):
            return es.enter_context(nc.sbuf_tensor(name, shape, dt))
        ident = sb("ident", [128, 128])
        pv = sb("pv", [128, 64])
        esink = sb("esink", [128, 8])
        ones_col = sb("ones_col", [128, 2])
        iota16 = sb("iota16", [128, 16])
        csp = sb("csp", [128, 4])
        ARENA_N = 48384 + 512 + 512 + 128 + 2048 + 128
        arena_t = sb("arena", [128, ARENA_N])
        ps = es.enter_context(nc.psum_tensor("ps", [128, 8, 512], F32))
        T = Trk(nc, es, kq=16)
        block = es.enter_context(nc.Block())
        AR = Arena(arena_t, ARENA_N)
        wa = AR.alloc(4, 128)
        wx = AR.alloc(4, 128)
        bones = AR.alloc(128)
        biasT = AR.alloc(2048)
        stag = AR.alloc(128)

        bank_ctr = [0]
        nbanks = [8]

        def nb():
            b = bank_ctr[0] % nbanks[0]
            bank_ctr[0] += 1
            return b

        def PB(b):
            return ("ps", b)

        def col(c):
            return pv[:, c:c + 1]

        T.op("pool", lambda g: g.memset(ident[:], 0.0), w=["ident"])
        T.op("pool", lambda g: g.affine_select(out=ident[:], in_=ident[:], pattern=[[-1, 128]],
                                               compare_op=ALU.not_equal, fill=1.0, base=0, channel_multiplier=1),
             r=["ident"], w=["ident"])
        T.op("pool", lambda g: g.memset(wa[:], 0.0), w=["wa"])
        T.op("pool", lambda g: g.memset(wx[:], 0.0), w=["wx"])
        T.op("pool", lambda g: g.memset(bones[:], 0.0), w=["bones"])
        T.op("pool", lambda g: g.memset(bones[0:64, 0:64], 1.0), r=["bones"], w=["bones"])
        T.op("pool", lambda g: g.memset(bones[64:128, 64:128], 1.0), r=["bones"], w=["bones"])
        T.op("pool", lambda g: g.memset(ones_col[:], 1.0), w=["ones_col"])
        T.op("pool", lambda g: g.iota(iota16[:], pattern=[[1, 16]], base=0, channel_multiplier=0,
                                      allow_small_or_imprecise_dtypes=True), w=["iota16"])
        T.op("pool", lambda g: g.memset(stag, 0.0), w=["stag"])
        T.dma("sp", lambda s: s.dma_start(out=stag[0:NROWS, :], in_=vecs_d), r=["stag"], w=["stag"])
        for n in range(8):
            c, p0 = n // 2, (n % 2) * 64
            T.dma("sp", lambda s, n=n, c=c, p0=p0: s.dma_start(out=wa[p0:p0 + 64, c, p0:p0 + 64], in_=wga_d[n]),
                  r=[], w=["wa"])
            T.dma("sp", lambda s, n=n, c=c, p0=p0: s.dma_start(out=wx[p0:p0 + 64, c, p0:p0 + 64], in_=wgx_d[n]),
                  r=[], w=["wx"])
        T.dma("sp", lambda s: s.dma_start(out=esink[:], in_=sinks_d[0, :].partition_broadcast(128)), w=["esink"])
        T.dma("sp", lambda s: s.dma_start(out=biasT[:], in_=bias_d), w=["biasT"])

        w_in_sb = AR.alloc(8, 1792)
        w_out_sb = AR.alloc(8, 1024)
        maskb = AR.alloc(2048)
        xt = [AR.alloc(1024) for _ in range(2)]
        junk = AR.alloc(1024)
        xs = AR.alloc(1024)
        hT = AR.alloc(8, 128)
        xbh = AR.alloc(4, 131)
        xc = AR.alloc(4, 128)
        rr = AR.alloc(4, 128)
        ig = AR.alloc(4, 128)
        aa = AR.alloc(4, 128)
        na2 = AR.alloc(512)
        sq = AR.alloc(512)
        gi = AR.alloc(512)
        bb = AR.alloc(4, 128)
        hh = [AR.alloc(4, 128) for _ in range(2)]
        gg = AR.alloc(512)
        yy = AR.alloc(4, 128)
        y2 = AR.alloc(4, 128)
        yg = AR.alloc(4, 128)
        q2 = AR.alloc(512)
        k2 = AR.alloc(128)
        sdq = AR.alloc(512)
        rq = AR.alloc(512)
        sdk = AR.alloc(128)
        rk = AR.alloc(128)
        qT = AR.alloc(4, 128)
        kT = [AR.alloc(128) for _ in range(2)]
        kTz = [[AR.alloc(128) for _ in range(2)] for _ in range(2)]
        gm = AR.alloc(2)
        qs = AR.alloc(512)
        ks = AR.alloc(128)
        vext = [AR.alloc(2, 65) for _ in range(2)]
        tmpS = [AR.alloc(512) for _ in range(4)]
        PT = [AR.alloc(512) for _ in range(4)]
        den = AR.alloc(8)
        rden = AR.alloc(8)
        yatt = AR.alloc(8, 64)
        yattT = AR.alloc(4, 128)
        ss1 = AR.alloc(2)
        sd1 = AR.alloc(2)
        rs1 = AR.alloc(2)
        ss2 = AR.alloc(2)
        sd2 = AR.alloc(2)
        rs2 = AR.alloc(2)
        tl = AR.alloc(4)
        print("[kernel] phase1 arena used", AR.off, "of", ARENA_N)

        w_in_v = w_in_d.rearrange("(c p) n -> p c n", p=128)
        for c in range(8):
            T.dma("sp", lambda s, c=c: s.dma_start(out=w_in_sb[:, c, :], in_=w_in_v[:, c, :]), w=[("w_in", c)])
        w_out_v = w_out_d.rearrange("(c p) n -> p c n", p=128)
        for c in range(8):
            T.dma("sp", lambda s, c=c: s.dma_start(out=w_out_sb[:, c, :], in_=w_out_v[:, c, :]), w=[("w_out", c)])
        T.dma("sp", lambda s: s.dma_start(out=maskb, in_=mask_d), w=["maskb"])

        b0 = nb()
        T.op("pe", lambda t: t.transpose(ps[:, b0, 0:128], stag, ident[:]),
             r=["stag", "ident"], w=[PB(b0)])
        T.op("dve", lambda v: v.tensor_copy(out=pv[:, 0:NROWS], in_=ps[:, b0, 0:NROWS]), r=[PB(b0)], w=["pv"])
        T.op("act", lambda a: a.activation(out=tl, in_=pv[:, R_L:R_L + 4], func=AF.Exp, scale=-1.0), r=["pv"], w=["tl"])
        T.op("act", lambda a: a.activation(out=tl, in_=tl, func=AF.Ln, bias=1.0), r=["tl"], w=["tl"])
        T.op("dve", lambda v: v.tensor_scalar(out=csp[:], in0=tl, scalar1=-8.0, scalar2=None, op0=ALU.mult),
             r=["tl"], w=["csp"])
        T.op("act", lambda a: a.activation(out=esink[:], in_=esink[:], func=AF.Exp), r=["esink"], w=["esink"])
        T.op("dve", lambda v: v.tensor_tensor(out=biasT[:], in0=biasT[:], in1=maskb, op=ALU.add),
             r=["biasT", "maskb"], w=["biasT"])
        T.op("pool", lambda g: g.memset(xbh[:, :, 0:3], 0.0), w=["xbh"])
        T.op("pool", lambda g: g.memset(gm, 0.0), w=["gm"])
        T.op("pool", lambda g: g.memset(gm[0:64, 0:1], 1.0), r=["gm"], w=["gm"])
        T.op("pool", lambda g: g.memset(gm[64:128, 1:2], 1.0), r=["gm"], w=["gm"])
        for i in range(2):
            T.op("pool", lambda g, i=i: g.memset(vext[i][:, :, 64:65], 1.0), w=[("vext", i)])

        BK_PT, bA, bB, bC, bD = 0, 2, 3, 4, 5
        back_banks = [6, 7, 1]
        bbc = [0]

        def nbk():
            b_ = back_banks[bbc[0] % 3]
            bbc[0] += 1
            return b_

        def front(t):
            X = xt[t % 2]
            XT = ("xt", t % 2)
            T.dma("sp", lambda s, t=t, X=X: s.dma_start(out=X, in_=x_d[t * 128:(t + 1) * 128, :]), w=[XT])
            T.op("dve", lambda v, X=X: v.scalar_tensor_tensor(out=junk, in0=X, scalar=1.0, in1=X, op0=ALU.mult,
                                                             op1=ALU.mult, accum_out=ss1[:, 0:1]),
                 r=[XT], w=["junk", "ss1"])
            T.op("act", lambda a: a.activation(out=sd1[:, 0:1], in_=ss1[:, 0:1], func=AF.Sqrt, scale=1.0 / DM, bias=EPS),
                 r=["ss1"], w=["sd1"])
            T.op("dve", lambda v: v.reciprocal(out=rs1[:, 0:1], in_=sd1[:, 0:1]), r=["sd1"], w=["rs1"])
            T.op("dve", lambda v, X=X: v.tensor_scalar(out=xs, in0=X, scalar1=rs1[:, 0:1], scalar2=None, op0=ALU.mult),
                 r=[XT, "rs1"], w=["xs"])
            for half in range(2):
                for c in range(half * 4, half * 4 + 4):
                    T.op("pe", lambda te, c=c: te.transpose(ps[:, BK_PT, (c % 4) * 128:(c % 4 + 1) * 128],
                                                             xs[:, c * 128:(c + 1) * 128], ident[:]),
                         r=["xs", "ident"], w=[PB(BK_PT)])
                for c in range(half * 4, half * 4 + 4):
                    T.op("act", lambda a, c=c: a.activation(out=hT[:, c, :], in_=ps[:, BK_PT, (c % 4) * 128:(c % 4 + 1) * 128],
                                                            func=AF.Identity, scale=col(R_GMIX + c)),
                         r=[PB(BK_PT), "pv"], w=[("hT", c)])
                yield
            for (bk, base) in ((bA, 0), (bB, 512), (bC, 1024)):
                for oc in range(4):
                    for kc in range(8):
                        T.op("pe", lambda te, bk=bk, base=base, oc=oc, kc=kc: te.matmul(
                            ps[:, bk, oc * 128:(oc + 1) * 128], lhsT=w_in_sb[:, kc, base + oc * 128: base + (oc + 1) * 128],
                            rhs=hT[:, kc, :], start=(kc == 0), stop=(kc == 7)),
                            r=[("w_in", kc), ("hT", kc)], w=[PB(bk)])
                    yield
            for kc in range(8):
                T.op("pe", lambda te, kc=kc: te.matmul(ps[:, bD, 0:128], lhsT=w_in_sb[:, kc, 1536:1664], rhs=hT[:, kc, :],
                                                       start=(kc == 0), stop=(kc == 7)),
                     r=[("w_in", kc), ("hT", kc)], w=[PB(bD)])
            yield
            for kc in range(8):
                T.op("pe", lambda te, kc=kc: te.matmul(ps[:, bD, 128:256], lhsT=hT[:, kc, :], rhs=w_in_sb[:, kc, 1664:1792],
                                                       start=(kc == 0), stop=(kc == 7)),
                     r=[("w_in", kc), ("hT", kc)], w=[PB(bD)])
            yield

        def back(t, nxt):
            X = xt[t % 2]
            XT = ("xt", t % 2)

            def step(n=2):
                for _ in range(n):
                    if next(nxt, "done") == "done":
                        break

            if t > 0:
                T.op("dve", lambda v: v.tensor_copy(out=xbh[:, :, 0:3], in_=xbh[:, :, 128:131]), r=["xbh"], w=["xbh"])
            T.op("act", lambda a: a.activation(out=xbh[:, :, 3:131], in_=ps[:, bA, :].rearrange("p (c t) -> p c t", c=4),
                                               func=AF.Identity), r=[PB(bA)], w=["xbh"])
            T.op("act", lambda a: a.activation(out=gg, in_=ps[:, bB, :], func=AF.Gelu_apprx_tanh), r=[PB(bB)], w=["gg"])
            T.op("act", lambda a: a.activation(out=q2, in_=ps[:, bC, :], func=AF.Square), r=[PB(bC)], w=["q2"])
            T.op("act", lambda a: a.activation(out=qs, in_=ps[:, bC, :], func=AF.Identity, scale=col(R_QG)), r=[PB(bC), "pv"], w=["qs"])
            T.op("act", lambda a: a.activation(out=k2, in_=ps[:, bD, 0:128], func=AF.Square), r=[PB(bD)], w=["k2"])
            T.op("act", lambda a: a.activation(out=ks, in_=ps[:, bD, 0:128], func=AF.Identity, scale=col(R_KG)), r=[PB(bD), "pv"], w=["ks"])
            VX = vext[t % 2]
            T.op("act", lambda a, VX=VX: a.activation(out=VX[:, :, 0:64], in_=ps[:, bD, 128:256].rearrange("p (g d) -> p g d", g=2),
                                                      func=AF.Identity), r=[PB(bD)], w=[("vext", t % 2)])
            step(2)
            for c in range(4):
                T.op("dve", lambda v, c=c: v.tensor_scalar(out=xc[:, c, :], in0=xbh[:, c, 3:131], scalar1=col(R_CW + 3 * 4 + c),
                                                           scalar2=col(R_CB + c), op0=ALU.mult, op1=ALU.add),
                     r=["xbh", "pv"], w=[("xc", c)])
                for tap in range(3):
                    T.op("dve", lambda v, c=c, tap=tap: v.scalar_tensor_tensor(
                        out=xc[:, c, :], in0=xbh[:, c, tap:tap + 128], scalar=col(R_CW + tap * 4 + c), in1=xc[:, c, :],
                        op0=ALU.mult, op1=ALU.add), r=["xbh", "pv", ("xc", c)], w=[("xc", c)])
            bE, bF = nbk(), nbk()
            for c in range(4):
                T.op("pe", lambda te, c=c: te.matmul(ps[:, bE, c * 128:(c + 1) * 128], lhsT=wa[:, c, :], rhs=xc[:, c, :],
                                                     start=True, stop=True), r=["wa", ("xc", c)], w=[PB(bE)])
            for c in range(4):
                T.op("pe", lambda te, c=c: te.matmul(ps[:, bF, c * 128:(c + 1) * 128], lhsT=wx[:, c, :], rhs=xc[:, c, :],
                                                     start=True, stop=True), r=["wx", ("xc", c)], w=[PB(bF)])
            step(2)
            for c in range(4):
                T.op("act", lambda a, c=c: a.activation(out=rr[:, c, :], in_=ps[:, bE, c * 128:(c + 1) * 128], func=AF.Sigmoid,
                                                        bias=col(R_BA + c)), r=[PB(bE), "pv"], w=[("rr", c)])
            for c in range(4):
                T.op("act", lambda a, c=c: a.activation(out=ig[:, c, :], in_=ps[:, bF, c * 128:(c + 1) * 128], func=AF.Sigmoid,
                                                        bias=col(R_BX + c)), r=[PB(bF), "pv"], w=[("ig", c)])
            for c in range(4):
                T.op("act", lambda a, c=c: a.activation(out=aa[:, c, :], in_=rr[:, c, :], func=AF.Exp, scale=csp[:, c:c + 1]),
                     r=[("rr", c), "csp"], w=[("aa", c)])
            aaf = aa.rearrange("p c t -> p (c t)")
            T.op("dve", lambda v: v.scalar_tensor_tensor(out=na2, in0=aaf, scalar=-1.0, in1=aaf, op0=ALU.mult, op1=ALU.mult),
                 r=[("aa", c) for c in range(4)], w=["na2"])
            T.op("act", lambda a: a.activation(out=sq, in_=na2, func=AF.Sqrt, bias=1.0), r=["na2"], w=["sq"])
            T.op("dve", lambda v: v.tensor_tensor(out=gi, in0=ig.rearrange("p c t -> p (c t)"),
                                                  in1=xc.rearrange("p c t -> p (c t)"), op=ALU.mult),
                 r=[("ig", c) for c in range(4)] + [("xc", c) for c in range(4)], w=["gi"])
            T.op("dve", lambda v: v.tensor_tensor(out=bb.rearrange("p c t -> p (c t)"), in0=sq, in1=gi, op=ALU.mult),
                 r=["sq", "gi"], w=["bb"])
            H = hh[t % 2]
            Hp = hh[(t - 1) % 2]
            for c in range(4):
                init = Hp[:, c, 127:128] if t > 0 else 0.0
                T.op("dve", lambda v, c=c, init=init, H=H: v.tensor_tensor_scan(out=H[:, c, :], data0=aa[:, c, :], data1=bb[:, c, :],
                                                                                initial=init, op0=ALU.mult, op1=ALU.add),
                     r=[("aa", c), "bb", ("hh", (t - 1) % 2)], w=[("hh", t % 2)])
            T.op("dve", lambda v, H=H: v.tensor_tensor(out=yy.rearrange("p c t -> p (c t)"), in0=H.rearrange("p c t -> p (c t)"),
                                                       in1=gg, op=ALU.mult), r=[("hh", t % 2), "gg"], w=["yy"])
            T.op("pool", lambda g: g.tensor_tensor(out=y2.rearrange("p c t -> p (c t)"), in0=yy.rearrange("p c t -> p (c t)"),
                                                   in1=yy.rearrange("p c t -> p (c t)"), op=ALU.mult), r=["yy"], w=["y2"])
            for c in range(4):
                T.op("dve", lambda v, c=c: v.tensor_scalar(out=yg[:, c, :], in0=yy[:, c, :], scalar1=col(R_GLRU + c), scalar2=None,
                                                           op0=ALU.mult), r=["yy", "pv"], w=[("yg", c)])

            bG, bH = nbk(), nbk()
            T.op("pe", lambda te: te.matmul(ps[:, bG, :], lhsT=bones[:], rhs=q2, start=True, stop=True),
                 r=["bones", "q2"], w=[PB(bG)])
            T.op("pe", lambda te: te.matmul(ps[:, bH, 0:128], lhsT=bones[:], rhs=k2, start=True, stop=True),
                 r=["bones", "k2"], w=[PB(bH)])
            step(2)
            T.op("act", lambda a: a.activation(out=sdq, in_=ps[:, bG, :], func=AF.Sqrt, scale=1.0 / 64, bias=EPS),
                 r=[PB(bG)], w=["sdq"])
            T.op("act", lambda a: a.activation(out=sdk, in_=ps[:, bH, 0:128], func=AF.Sqrt, scale=1.0 / 64, bias=EPS),
                 r=[PB(bH)], w=["sdk"])
            T.op("dve", lambda v: v.reciprocal(out=rq, in_=sdq), r=["sdq"], w=["rq"])
            T.op("dve", lambda v: v.reciprocal(out=rk, in_=sdk), r=["sdk"], w=["rk"])
            T.op("dve", lambda v: v.tensor_tensor(out=qT.rearrange("p c t -> p (c t)"), in0=qs, in1=rq, op=ALU.mult),
                 r=["qs", "rq"], w=["qT"])
            KT = kT[t % 2]
            T.op("dve", lambda v, KT=KT: v.tensor_tensor(out=KT, in0=ks, in1=rk, op=ALU.mult),
                 r=["ks", "rk"], w=[("kT", t % 2)])
            for g in range(2):
                T.op("pool", lambda ge, g=g, KT=KT: ge.tensor_scalar(out=kTz[t % 2][g], in0=KT, scalar1=gm[:, g:g + 1], scalar2=1.0,
                                                                    op0=ALU.mult, op1=ALU.mult),
                     r=[("kT", t % 2), "gm"], w=[("kTz", t % 2, g)])
            ktiles = ([t - 1] if t > 0 else []) + [t]
            for g in range(2):
                for ci, kt in enumerate(ktiles):
                    b = nbk()
                    cc = 1 if kt == t else 0
                    i4 = g * 2 + ci
                    boff = (g * 2 + cc) * 512
                    T.op("pe", lambda te, g=g, kt=kt, b=b: te.matmul(
                        ps[:, b, :], lhsT=kTz[kt % 2][g], rhs=qT,
                        start=True, stop=True), r=[("kTz", kt % 2, g), "qT"], w=[PB(b)])
                    T.op("dve", lambda v, b=b, i4=i4, boff=boff: v.scalar_tensor_tensor(
                        out=tmpS[i4], in0=ps[:, b, :], scalar=SCALE, in1=biasT[:, boff:boff + 512], op0=ALU.mult, op1=ALU.add),
                        r=[PB(b), "biasT"], w=[("tmpS", i4)])
                    T.op("act", lambda a, i4=i4: a.activation(out=PT[i4], in_=tmpS[i4], func=AF.Exp),
                         r=[("tmpS", i4)], w=[("PT", i4)])
            step(2)
            bO = [nbk(), nbk()]
            for g in range(2):
                for j in range(4):
                    for ci, kt in enumerate(ktiles):
                        i4 = g * 2 + ci
                        T.op("pe", lambda te, g=g, j=j, ci=ci, kt=kt, i4=i4: te.matmul(
                            ps[:, bO[g], j * 128:j * 128 + 65], lhsT=PT[i4][:, j * 128:(j + 1) * 128], rhs=vext[kt % 2][:, g, :],
                            start=(ci == 0), stop=(ci == len(ktiles) - 1)),
                            r=[("PT", i4), ("vext", kt % 2)], w=[PB(bO[g])])
            bT_ss = nbk()
            for c in range(4):
                T.op("pe", lambda te, c=c: te.matmul(ps[:, bT_ss, 300:301], lhsT=y2[:, c, :], rhs=ones_col[:, 0:1],
                                                     start=(c == 0), stop=(c == 3)), r=["y2", "ones_col"], w=[PB(bT_ss)])
            T.op("act", lambda a: a.activation(out=ss2[:, 0:1], in_=ps[:, bT_ss, 300:301], func=AF.Identity), r=[PB(bT_ss)], w=["ss2a"])
            step(3)
            for g in range(2):
                ov = ps[:, bO[g], :].rearrange("p (j e) -> p j e", j=4)
                T.op("dve", lambda v, g=g, ov=ov: v.tensor_tensor(out=den[:, g * 4:(g + 1) * 4], in0=ov[:, :, 64],
                                                                  in1=esink[:, g * 4:(g + 1) * 4], op=ALU.add),
                     r=[PB(bO[g]), "esink"], w=[("den", g)])
            T.op("dve", lambda v: v.reciprocal(out=rden, in_=den), r=[("den", 0), ("den", 1)], w=["rden"])
            for g in range(2):
                ov = ps[:, bO[g], :].rearrange("p (j e) -> p j e", j=4)
                T.op("dve", lambda v, g=g, ov=ov: v.tensor_tensor(
                    out=yatt[:, g * 4:(g + 1) * 4, :], in0=ov[:, :, 0:64],
                    in1=rden[:, g * 4:(g + 1) * 4].unsqueeze(2).broadcast_to([128, 4, 64]), op=ALU.mult),
                    r=[PB(bO[g]), "rden"], w=[("yatt", g)])
            yaf = yatt.rearrange("p h d -> p (h d)")
            T.op("dve", lambda v: v.scalar_tensor_tensor(out=junk[:, 0:512], in0=yaf, scalar=1.0, in1=yaf, op0=ALU.mult,
                                                         op1=ALU.mult, accum_out=ss2[:, 1:2]),
                 r=[("yatt", 0), ("yatt", 1)], w=["junk", "ss2b"])
            bT = nbk()
            for c in range(4):
                T.op("pe", lambda te, c=c: te.transpose(ps[:, bT, c * 128:(c + 1) * 128], yaf[:, c * 128:(c + 1) * 128], ident[:]),
                     r=[("yatt", 0), ("yatt", 1), "ident"], w=[PB(bT)])
            step(2)
            for c in range(4):
                T.op("act", lambda a, c=c: a.activation(out=yattT[:, c, :], in_=ps[:, bT, c * 128:(c + 1) * 128], func=AF.Identity,
                                                        scale=col(R_GATT + c)), r=[PB(bT), "pv"], w=[("yattT", c)])
            T.op("act", lambda a: a.activation(out=sd2, in_=ss2, func=AF.Sqrt, scale=1.0 / 512, bias=EPS),
                 r=["ss2a", "ss2b"], w=["sd2"])
            T.op("dve", lambda v: v.reciprocal(out=rs2, in_=sd2), r=["sd2"], w=["rs2"])
            bL = [nbk(), nbk()]
            for hf in range(2):
                for c in range(4):
                    T.op("pe", lambda te, hf=hf, c=c: te.matmul(ps[:, bL[hf], :], lhsT=yg[:, c, :],
                                                                rhs=w_out_sb[:, c, hf * 512:(hf + 1) * 512],
                                                                start=(c == 0), stop=(c == 3)),
                         r=[("yg", c), ("w_out", c)], w=[PB(bL[hf])])
            for hf in range(2):
                T.op("dve", lambda v, hf=hf, X=X: v.scalar_tensor_tensor(
                    out=X[:, hf * 512:(hf + 1) * 512], in0=ps[:, bL[hf], :], scalar=rs2[:, 0:1], in1=X[:, hf * 512:(hf + 1) * 512],
                    op0=ALU.mult, op1=ALU.add), r=[PB(bL[hf]), "rs2", XT], w=[XT])
            step(2)
            bM = [nbk(), nbk()]
            for hf in range(2):
                for c in range(4):
                    T.op("pe", lambda te, hf=hf, c=c: te.matmul(ps[:, bM[hf], :], lhsT=yattT[:, c, :],
                                                                rhs=w_out_sb[:, 4 + c, hf * 512:(hf + 1) * 512],
                                                                start=(c == 0), stop=(c == 3)),
                         r=[("yattT", c), ("w_out", 4 + c)], w=[PB(bM[hf])])
            for hf in range(2):
                T.op("dve", lambda v, hf=hf, X=X: v.scalar_tensor_tensor(
                    out=X[:, hf * 512:(hf + 1) * 512], in0=ps[:, bM[hf], :], scalar=rs2[:, 1:2], in1=X[:, hf * 512:(hf + 1) * 512],
                    op0=ALU.mult, op1=ALU.add), r=[PB(bM[hf]), "rs2", XT], w=[XT])
            for _ in nxt:
                pass
            T.dma("sp", lambda s, t=t, X=X: s.dma_start(out=out_d[t * 128:(t + 1) * 128, :], in_=X), r=[XT], w=[("out", t)])

        for _ in front(0):
            pass
        for t in range(nt):
            back(t, front(t + 1) if t + 1 < nt else iter(()))

        if stage >= 2:
            T.barrier()
            AR.reset()
            nbanks[0] = 6 if int(os.environ.get("PE_EVERY", "0")) > 0 else 8
            NDG = 4
            PE_EVERY = int(os.environ.get("PE_EVERY", "0"))

            def alloc_rest(AR):
                g = {}
                g['wq_sb'] = AR.alloc(8, 2048)
                g['skT_sb'] = AR.alloc(16, 128)
                g['gffn_b'] = AR.alloc(1024)
                g['xn'] = [AR.alloc(1024) for _ in range(2)]
                g['xn2'] = [AR.alloc(1024) for _ in range(2)]
                g['xn2T'] = AR.alloc(8, 128)
                g['qpT'] = AR.alloc(16, 128)
                g['s_all'] = AR.alloc(16, 128)
                g['swk'] = AR.alloc(16, 128)
                g['cand'] = AR.alloc(8, 256)
                g['stop'] = AR.alloc(16, 16)
                g['sidx'] = AR.alloc(16, 16, dt=U32)
                g['sidxf'] = AR.alloc(16, 16)
                g['best'] = AR.alloc(8, 16)
                g['pos'] = AR.alloc(8, 16, dt=U32)
                g['pa'] = AR.alloc(128, dt=U32)
                g['pb_'] = AR.alloc(128, dt=U32)
                g['paf'] = AR.alloc(128)
                g['pbf'] = AR.alloc(128)
                g['i1f'] = AR.alloc(128)
                g['i2f'] = AR.alloc(128)
                g['idxf'] = AR.alloc(128)
                g['idx_i'] = [AR.alloc(128, dt=I32) for _ in range(2)]
                g['bsub'] = AR.alloc(8, 16)
                g['eb'] = AR.alloc(8, 16)
                g['gw'] = AR.alloc(8, 16)
                g['AR_extra_gw'] = AR.alloc(8, 16)
                g['dgs'] = [AR.alloc(128) for _ in range(NDG)]
                g['se'] = AR.alloc(8)
                g['rse'] = AR.alloc(8)
                g['actv'] = AR.alloc(128)
                g['gact'] = AR.alloc(128)
                g['wv'] = [AR.alloc(128) for _ in range(2)]
                g['ss3'] = AR.alloc(2)
                g['sd3'] = AR.alloc(2)
                g['rs3'] = AR.alloc(2)
                return g
            class _Dry(Arena):
                def alloc(self, *shape, dt=None):
                    n = 1
                    for s_ in shape:
                        n *= s_
                    self.off += n
                    return None
            dry = _Dry(None, 10 ** 9)
            alloc_rest(dry)
            NS = min(int(os.environ.get("NSLAB", "16")), (ARENA_N - dry.off) // 1024)
            assert NS >= 4, NS
            G = alloc_rest(AR)
            slabs = [AR.alloc(1024) for _ in range(NS)]
            junk_ctr = [0]
            dg_ctr = [0]

            def nj():
                i = junk_ctr[0] % 2
                junk_ctr[0] += 1
                return junk2r[i], ("junk2", i)
            wq_sb = G['wq_sb']
            skT_sb = G['skT_sb']
            gffn_b = G['gffn_b']
            xn = G['xn']
            xn2 = G['xn2']
            xn2T = G['xn2T']
            qpT = G['qpT']
            s_all = G['s_all']
            swk = G['swk']
            cand = G['cand']
            stop = G['stop']
            sidx = G['sidx']
            sidxf = G['sidxf']
            best = G['best']
            pos = G['pos']
            pa = G['pa']
            pb_ = G['pb_']
            paf = G['paf']
            pbf = G['pbf']
            i1f = G['i1f']
            i2f = G['i2f']
            idxf = G['idxf']
            idx_i = G['idx_i']
            bsub = G['bsub']
            eb = G['eb']
            gw = G['gw']
            AR_extra_gw = G['AR_extra_gw']
            dgs = G['dgs']
            se = G['se']
            rse = G['rse']
            actv = G['actv']
            gact = G['gact']
            wv = G['wv']
            ss3 = G['ss3']
            sd3 = G['sd3']
            rs3 = G['rs3']
            print("[kernel] phase2 arena used", AR.off, "of", ARENA_N, "slabs", NS)
            cwk = qpT.rearrange("p a b -> p (a b)").rearrange("p (h n) -> p h n", h=8)
            eqa = s_all.rearrange("p a b -> p (a b)").rearrange("p (k a) -> p k a", a=16)
            eqb = swk.rearrange("p a b -> p (a b)").rearrange("p (k a) -> p k a", a=16)

            wq_v = wq_d.rearrange("(c p) n -> p c n", p=128)
            for c in range(8):
                T.dma("sp", lambda s, c=c: s.dma_start(out=wq_sb[:, c, :], in_=wq_v[:, c, :]), w=[("wq", c)])
            T.dma("sp", lambda s: s.dma_start(out=skT_sb, in_=skT_d.rearrange("h d n -> d h n")), w=["skT"])
            T.dma("sp", lambda s: s.dma_start(out=gffn_b, in_=gffn_d[0, :].partition_broadcast(128)), w=["gffn"])
            slab_ctr = [0]
            for i in range(2):
                T.op("pool", lambda g, i=i: g.memset(idx_i[i], 0), w=[("idx", i)])

            gw2 = [gw, AR_extra_gw]

            def prep(t):
                XN = xn[t % 2]
                XNT = ("xn", t % 2)
                X2 = xn2[t % 2]
                X2T = ("xn2", t % 2)
                GW = gw2[t % 2]
                GWT = ("gw", t % 2)
                T.dma("sp", lambda s, t=t, XN=XN: s.dma_start(out=XN, in_=out_d[t * 128:(t + 1) * 128, :]),
                      r=[("out", t)], w=[XNT])
                T.op("dve", lambda v, XN=XN, X2=X2: v.scalar_tensor_tensor(out=X2, in0=XN, scalar=1.0, in1=XN, op0=ALU.mult,
                                                                          op1=ALU.mult, accum_out=ss3[:, 0:1]),
                     r=[XNT], w=["ss3", X2T])
                yield
                T.op("act", lambda a: a.activation(out=sd3[:, 0:1], in_=ss3[:, 0:1], func=AF.Sqrt, scale=1.0 / DM, bias=EPS),
                     r=["ss3"], w=["sd3"])
                T.op("dve", lambda v: v.reciprocal(out=rs3[:, 0:1], in_=sd3[:, 0:1]), r=["sd3"], w=["rs3"])
                yield
                T.op("dve", lambda v, XN=XN, X2=X2: v.scalar_tensor_tensor(out=X2, in0=XN, scalar=rs3[:, 0:1], in1=gffn_b,
                                                                          op0=ALU.mult, op1=ALU.mult),
                     r=[XNT, "rs3", "gffn"], w=[X2T])
                yield
                pt = [nb(), nb()]
                for c in range(8):
                    T.op("pe", lambda te, c=c, X2=X2: te.transpose(ps[:, pt[c // 4], (c % 4) * 128:(c % 4 + 1) * 128],
                                                                  X2[:, c * 128:(c + 1) * 128], ident[:]),
                         r=[X2T, "ident"], w=[PB(pt[c // 4])])
                for i in range(2):
                    T.op("act", lambda a, i=i: a.activation(out=xn2T[:, i * 4:(i + 1) * 4, :],
                                                            in_=ps[:, pt[i], :].rearrange("p (c t) -> p c t", c=4), func=AF.Identity),
                         r=[PB(pt[i])], w=[("xn2T", i)])
                bq = [nb(), nb(), nb(), nb()]
                for hc in range(16):
                    for kc in range(8):
                        T.op("pe", lambda te, hc=hc, kc=kc: te.matmul(
                            ps[:, bq[hc // 4], (hc % 4) * 128:(hc % 4 + 1) * 128], lhsT=wq_sb[:, kc, hc * 128:(hc + 1) * 128],
                            rhs=xn2T[:, kc, :], start=(kc == 0), stop=(kc == 7)),
                            r=[("wq", kc), ("xn2T", kc // 4)], w=[PB(bq[hc // 4])])
                for i in range(4):
                    T.op("act", lambda a, i=i: a.activation(out=qpT[:, i * 4:(i + 1) * 4, :],
                                                            in_=ps[:, bq[i], :].rearrange("p (c t) -> p c t", c=4), func=AF.Identity),
                         r=[PB(bq[i])], w=[("qpT", i)])
                bs_ = [nb(), nb(), nb(), nb()]
                for hc in range(16):
                    T.op("pe", lambda te, hc=hc: te.matmul(ps[:, bs_[hc // 4], (hc % 4) * 128:(hc % 4 + 1) * 128],
                                                           lhsT=qpT[:, hc, :], rhs=skT_sb[:, hc, :], start=True, stop=True),
                         r=[("qpT", hc // 4), "skT"], w=[PB(bs_[hc // 4])])
                for i in range(4):
                    T.op("act", lambda a, i=i: a.activation(out=s_all[:, i * 4:(i + 1) * 4, :],
                                                            in_=ps[:, bs_[i], :].rearrange("p (c t) -> p c t", c=4), func=AF.Identity),
                         r=[PB(bs_[i])], w=[("s_all", i)])
                for hc in range(16):
                    SA = ("s_all", hc // 4)
                    SW = ("swk", hc)
                    ST0, ST1 = ("stop", hc, 0), ("stop", hc, 1)
                    T.op("dve", lambda v, hc=hc: v.max(out=stop[:, hc, 0:8], in_=s_all[:, hc, :]), r=[SA], w=[ST0])
                    yield
                    T.op("dve", lambda v, hc=hc: v.max_index(out=sidx[:, hc, 0:8], in_max=stop[:, hc, 0:8], in_values=s_all[:, hc, :]),
                         r=[SA, ST0], w=[("sidx", hc, 0)])
                    yield
                    T.op("dve", lambda v, hc=hc: v.match_replace(out=swk[:, hc, :], in_to_replace=stop[:, hc, 0:8],
                                                                 in_values=s_all[:, hc, :], imm_value=-1e30),
                         r=[SA, ST0], w=[SW])
                    yield
                    T.op("dve", lambda v, hc=hc: v.max(out=stop[:, hc, 8:16], in_=swk[:, hc, :]), r=[SW], w=[ST1])
                    yield
                    T.op("dve", lambda v, hc=hc: v.max_index(out=sidx[:, hc, 8:16], in_max=stop[:, hc, 8:16], in_values=swk[:, hc, :]),
                         r=[SW, ST1], w=[("sidx", hc, 1)])
                    yield
                STALL = [("stop", hc, i) for hc in range(16) for i in range(2)]
                SIALL = [("sidx", hc, i) for hc in range(16) for i in range(2)]
                T.op("dve", lambda v: v.tensor_copy(out=sidxf, in_=sidx), r=SIALL, w=["sidxf"])
                yield
                st4 = stop.rearrange("p (h c) k -> p h c k", c=2)
                sf4 = sidxf.rearrange("p (h c) k -> p h c k", c=2)
                T.op("dve", lambda v: v.tensor_tensor(out=cand.rearrange("p h (a b) -> p h a b", a=16),
                                                      in0=st4[:, :, 0, :].unsqueeze(3).broadcast_to([128, 8, 16, 16]),
                                                      in1=st4[:, :, 1, :].unsqueeze(2).broadcast_to([128, 8, 16, 16]), op=ALU.add),
                     r=STALL, w=["cand"])
                yield
                for h in range(8):
                    CWh = ("cwk", h)
                    B0, B1 = ("best", h, 0), ("best", h, 1)
                    T.op("dve", lambda v, h=h: v.max(out=best[:, h, 0:8], in_=cand[:, h, :]), r=["cand"], w=[B0])
                    yield
                    T.op("dve", lambda v, h=h: v.max_index(out=pos[:, h, 0:8], in_max=best[:, h, 0:8], in_values=cand[:, h, :]),
                         r=["cand", B0], w=[("pos", h, 0)])
                    yield
                    T.op("dve", lambda v, h=h: v.match_replace(out=cwk[:, h, :], in_to_replace=best[:, h, 0:8],
                                                               in_values=cand[:, h, :], imm_value=-1e30),
                         r=["cand", B0] + [("qpT", i) for i in range(4)], w=[CWh])
                    yield
                    T.op("dve", lambda v, h=h: v.max(out=best[:, h, 8:16], in_=cwk[:, h, :]), r=[CWh], w=[B1])
                    yield
                    T.op("dve", lambda v, h=h: v.max_index(out=pos[:, h, 8:16], in_max=best[:, h, 8:16], in_values=cwk[:, h, :]),
                         r=[CWh, B1], w=[("pos", h, 1)])
                    yield
                BALL = [("best", h, i) for h in range(8) for i in range(2)]
                PALL = [("pos", h, i) for h in range(8) for i in range(2)]
                CWALL = [("cwk", h) for h in range(8)]
                posf = pos.rearrange("p h k -> p (h k)")
                T.op("dve", lambda v: v.tensor_single_scalar(out=pa, in_=posf, scalar=4, op=ALU.logical_shift_right),
                     r=PALL, w=["pa"])
                T.op("dve", lambda v: v.tensor_single_scalar(out=pb_, in_=posf, scalar=15, op=ALU.bitwise_and),
                     r=PALL, w=["pb"])
                yield
                T.op("dve", lambda v: v.tensor_copy(out=paf, in_=pa), r=["pa"], w=["paf"])
                T.op("dve", lambda v: v.tensor_copy(out=pbf, in_=pb_), r=["pb"], w=["pbf"])
                yield
                EA = [("s_all", i) for i in range(4)]
                EB = [("swk", hc) for hc in range(16)]
                for (pf, eq, ET, ci, of, tok) in ((paf, eqa, EA, 0, i1f, "i1f"), (pbf, eqb, EB, 1, i2f, "i2f")):
                    T.op("dve", lambda v, pf=pf, eq=eq: v.tensor_tensor(out=eq, in0=pf.unsqueeze(2).broadcast_to([128, 128, 16]),
                                                                        in1=iota16[:].unsqueeze(1).broadcast_to([128, 128, 16]),
                                                                        op=ALU.is_equal),
                         r=["paf", "pbf", "iota16"], w=ET)
                    yield
                    eq4 = eq.rearrange("p (h k) a -> p h k a", h=8)
                    T.op("dve", lambda v, eq4=eq4, ci=ci: v.tensor_tensor(out=eq4, in0=eq4,
                                                                          in1=sf4[:, :, ci, :].unsqueeze(2).broadcast_to([128, 8, 16, 16]),
                                                                          op=ALU.mult),
                         r=ET + ["sidxf"], w=ET)
                    yield
                    T.op("dve", lambda v, eq=eq, of=of: v.tensor_reduce(out=of, in_=eq, axis=AX.X, op=ALU.add), r=ET, w=[tok])
                    yield
                T.op("dve", lambda v: v.scalar_tensor_tensor(out=idxf, in0=i1f, scalar=128.0, in1=i2f, op0=ALU.mult, op1=ALU.add),
                     r=["i1f", "i2f"], w=["idxf"])
                T.op("dve", lambda v: v.tensor_scalar(out=idxf, in0=idxf, scalar1=0.0, scalar2=float(NEXP - 1), op0=ALU.max, op1=ALU.min),
                     r=["idxf"], w=["idxf"])
                IDX = idx_i[t % 2]
                IDXT = ("idx", t % 2)
                T.op("dve", lambda v, IDX=IDX: v.tensor_copy(out=IDX, in_=idxf), r=["idxf"], w=[IDXT])
                yield
                T.op("dve", lambda v: v.tensor_tensor(out=bsub, in0=best, in1=best[:, :, 0:1].broadcast_to([128, 8, 16]),
                                                      op=ALU.subtract), r=BALL, w=["bsub"])
                T.op("act", lambda a: a.activation(out=eb, in_=bsub, func=AF.Exp), r=["bsub"], w=["eb"])
                T.op("dve", lambda v: v.tensor_reduce(out=se, in_=eb, axis=AX.X, op=ALU.add), r=["eb"], w=["se"])
                yield
                T.op("dve", lambda v: v.reciprocal(out=rse, in_=se), r=["se"], w=["rse"])
                T.op("dve", lambda v, GW=GW: v.tensor_tensor(out=GW, in0=eb, in1=rse.unsqueeze(2).broadcast_to([128, 8, 16]), op=ALU.mult),
                     r=["eb", "rse"], w=[GWT])
                yield

            def gather_phase(t, nxt, per_slot):
                XN = xn[t % 2]
                XNT = ("xn", t % 2)
                X2 = xn2[t % 2]
                X2T = ("xn2", t % 2)
                IDX = idx_i[t % 2]
                IDXT = ("idx", t % 2)
                GW = gw2[t % 2]
                GWT = ("gw", t % 2)

                credit = [0.0]

                def step():
                    credit[0] += per_slot
                    while credit[0] >= 1.0:
                        credit[0] -= 1.0
                        if next(nxt, "done") == "done":
                            break
                for hk in range(128):
                    si = slab_ctr[0] % NS
                    slab_ctr[0] += 1
                    SL = slabs[si]
                    SLT = ("slab", si)
                    T.dma("pool", lambda g, hk=hk, SL=SL, IDX=IDX: g.indirect_dma_start(
                        out=SL, out_offset=None, in_=eu_d, in_offset=bass.IndirectOffsetOnAxis(ap=IDX[:, hk:hk + 1], axis=0)),
                        r=[IDXT], w=[SLT])
                    T.op("dve", lambda v, hk=hk, SL=SL, X2=X2: v.scalar_tensor_tensor(
                        out=SL, in0=SL, scalar=1.0, in1=X2, op0=ALU.mult, op1=ALU.mult, accum_out=actv[:, hk:hk + 1]),
                        r=[SLT, X2T], w=[("actv", hk), SLT])
                    step()
                T.op("act", lambda a: a.activation(out=gact, in_=actv, func=AF.Gelu_apprx_tanh),
                     r=[("actv", hk) for hk in range(128)], w=["gact"])
                WV = wv[t % 2]
                WVT = ("wv", t % 2)
                T.op("dve", lambda v, WV=WV, GW=GW: v.tensor_tensor(out=WV, in0=gact, in1=GW.rearrange("p h k -> p (h k)"), op=ALU.mult),
                     r=["gact", GWT], w=[WVT])
                for hk in range(128):
                    si = slab_ctr[0] % NS
                    slab_ctr[0] += 1
                    SL = slabs[si]
                    SLT = ("slab", si)
                    T.dma("pool", lambda g, hk=hk, SL=SL, IDX=IDX: g.indirect_dma_start(
                        out=SL, out_offset=None, in_=ev_d, in_offset=bass.IndirectOffsetOnAxis(ap=IDX[:, hk:hk + 1], axis=0)),
                        r=[IDXT], w=[SLT])
                    if PE_EVERY > 0 and hk % PE_EVERY == 0:
                        di = dg_ctr[0] % NDG
                        dg_ctr[0] += 1
                        DG = dgs[di]
                        DGT = ("dg", di)
                        T.op("act", lambda a, hk=hk, DG=DG, WV=WV: a.activation(out=DG, in_=ident[:], func=AF.Identity,
                                                                            scale=WV[:, hk:hk + 1]),
                             r=["ident", WVT], w=[DGT])
                        last_pe = ((127 // max(PE_EVERY, 1)) * max(PE_EVERY, 1))
                        for hf in range(2):
                            T.op("pe", lambda te, hk=hk, hf=hf, DG=DG, SL=SL, last_pe=last_pe: te.matmul(
                                ps[:, 6 + hf, :], lhsT=DG, rhs=SL[:, hf * 512:(hf + 1) * 512], start=(hk == 0), stop=(hk == last_pe)),
                                r=[DGT, SLT], w=[PB(6 + hf)])
                    else:
                        T.op("dve", lambda v, hk=hk, SL=SL, XN=XN, WV=WV: v.scalar_tensor_tensor(
                            out=XN, in0=SL, scalar=WV[:, hk:hk + 1], in1=XN, op0=ALU.mult, op1=ALU.add),
                            r=[SLT, WVT, XNT], w=[XNT])
                    step()
                for hf in (range(2) if PE_EVERY > 0 else ()):
                    T.op("dve", lambda v, hf=hf, XN=XN: v.tensor_tensor(out=XN[:, hf * 512:(hf + 1) * 512], in0=ps[:, 6 + hf, :],
                                                                       in1=XN[:, hf * 512:(hf + 1) * 512], op=ALU.add),
                         r=[PB(6 + hf), XNT], w=[XNT])
                for _ in nxt:
                    pass
                T.dma("sp", lambda s, t=t, XN=XN: s.dma_start(out=out_d[t * 128:(t + 1) * 128, :], in_=XN),
                      r=[XNT], w=[("out", t)])

            for _ in prep(0):
                pass
            for t in range(nt):
                nxt = prep(t + 1) if t + 1 < nt else iter(())
                gather_phase(t, nxt, float(os.environ.get("PREP_RATE", "0.64")))

        T.finish()
        T.emit(block)
        build_nc.dbg = {"yy": yy.offset, "yatt": yatt.offset, "qT": qT.offset, "xbh": xbh.offset, "gg": gg.offset, "hT": hT.offset}
    return nc


def _t5_bucket(rel):
    n = np.maximum(rel, 0)
    max_exact = 16
    nf = np.maximum(n, 1).astype(np.float32)
    large = max_exact + (np.log(nf / np.float32(max_exact)) / np.float32(np.log(128 / max_exact))
                         * np.float32(32 - max_exact)).astype(np.int32)
    large = np.minimum(large, 31)
    return np.where(n < max_exact, n, large)


def _prep_shared(inp):
    f = lambda a: np.ascontiguousarray(np.asarray(a, dtype=np.float32))
    w_in = f(inp["w_in"])[0]
    qcols = np.arange(1024, 1536).reshape(2, 4, 64)
    perm = np.concatenate([np.arange(0, 1024), qcols.transpose(1, 0, 2).reshape(-1), np.arange(1536, 1792)])
    w_in_p = np.ascontiguousarray(w_in[:, perm])
    rows = []
    cw = f(inp["conv_w"])[0]
    rows.append(cw.reshape(4, 4, 128).reshape(16, 128))
    for k in ("conv_b", "b_gate_a", "b_gate_x", "lru_L", "lru_out_g", "attn_out_g"):
        rows.append(f(inp[k]).reshape(4, 128))
    rows.append(f(inp["ln_mix_g"]).reshape(8, 128))
    rows.append(np.tile(f(inp["q_norm_g"]).reshape(1, 64), (1, 2)))
    rows.append(np.tile(f(inp["k_norm_g"]).reshape(1, 64), (1, 2)))
    vecs = np.ascontiguousarray(np.concatenate(rows, axis=0))
    assert vecs.shape == (NROWS, 128)
    rel_bias = f(inp["rel_bias"])
    j = np.arange(128)[:, None]
    i = np.arange(128)[None, :]
    rel_cur = i - j
    rel_prev = 128 + i - j
    bias_band = np.zeros((128, 2, 2, 4, 128), np.float32)
    mask_band = np.zeros((128, 2, 2, 4, 128), np.float32)
    for c, rel, valid in ((0, rel_prev, rel_prev < 128), (1, rel_cur, rel_cur >= 0)):
        bk = _t5_bucket(rel)
        for g in range(2):
            for jj in range(4):
                bias_band[:, g, c, jj, :] = rel_bias[bk, g * 4 + jj]
                mask_band[:, g, c, jj, :] = np.where(valid, 0.0, MASKV)
    sk = f(inp["sub_keys"])[0]
    skT = np.ascontiguousarray(sk.reshape(16, 128, 128).transpose(0, 2, 1))
    return {
        "vecs": vecs, "w_in": w_in_p,
        "w_gate_a": f(inp["w_gate_a"])[0], "w_gate_x": f(inp["w_gate_x"])[0],
        "sinks": f(inp["sinks"]).reshape(1, 8), "w_out": f(inp["w_out"])[0],
        "ln_ffn_g": f(inp["ln_ffn_g"]).reshape(1, DM), "w_query": f(inp["w_query"])[0],
        "skT": skT, "expert_u": f(inp["expert_u"])[0], "expert_v": f(inp["expert_v"])[0],
        "bias_band": bias_band.reshape(128, 2048), "mask_band": mask_band.reshape(128, 2048),
    }


def kernel(**inputs):
    stage = int(os.environ.get("KSTAGE", "2"))
    shared = _prep_shared(inputs)
    x = np.ascontiguousarray(np.asarray(inputs["x"], dtype=np.float32))
    nc = build_nc(stage)
    in_maps = []
    for b in range(8):
        m = dict(shared)
        m["x"] = x[b]
        in_maps.append(m)
    res = run_bass_kernel_spmd(nc, in_maps, core_ids=list(range(8)))
    return np.stack([r["out"] for r in res.results], axis=0).astype(np.float32)
```

```python
import os
from contextlib import ExitStack
import numpy as np
import concourse.bass as bass
import concourse.mybir as mybir
from concourse.bass_utils import run_bass_kernel_spmd

F32 = mybir.dt.float32
I32 = mybir.dt.int32
U32 = mybir.dt.uint32
AF = mybir.ActivationFunctionType
ALU = mybir.AluOpType
AX = mybir.AxisListType

S = 2048
DM = 1024
NT = S // 128
EPS = 1e-6
SCALE = 64 ** -0.5
NEXP = 16384
MASKV = -200.0

R_CW, R_CB, R_BA, R_BX, R_L, R_GLRU, R_GATT, R_GMIX, R_QG, R_KG, NROWS = 0, 16, 20, 24, 28, 32, 36, 40, 48, 49, 50


class Trk:
    def __init__(self, nc, es, kq=12):
        self.nc = nc
        self.eng = {"pe": nc.tensor, "act": nc.scalar, "dve": nc.vector, "pool": nc.gpsimd, "sp": nc.sync}
        self.csem = {e: es.enter_context(nc.semaphore("c_" + e)) for e in ("pe", "act", "dve", "pool")}
        self.kq = kq
        self.dsem = {q: [es.enter_context(nc.semaphore("d_%s%d" % (q, i))) for i in range(kq)] for q in ("sp", "pool")}
        self.duse = {q: [0] * kq for q in self.dsem}
        self.dn = {q: 0 for q in self.dsem}
        self.prog = {e: [] for e in self.eng}
        self.cnt = {e: 0 for e in self.csem}
        self.waited = {e: {} for e in self.eng}
        self.last_w = {}
        self.readers = {}
        self.same = {"pe": False, "act": True, "dve": True, "pool": True, "sp": True}

    def _sem(self, k):
        return self.csem[k[1]] if k[0] == "c" else self.dsem[k[1]][k[2]]

    def _deps(self, r, w):
        deps = {}

        def add(k, v):
            if deps.get(k, 0) < v:
                deps[k] = v
        for t in r:
            if t in self.last_w:
                add(*self.last_w[t])
        for t in w:
            if t in self.last_w:
                add(*self.last_w[t])
            for k, v in self.readers.get(t, {}).items():
                add(k, v)
        return deps

    def _commit(self, ev, r, w):
        for t in w:
            self.last_w[t] = ev
            self.readers[t] = {}
        for t in r:
            d = self.readers.setdefault(t, {})
            if d.get(ev[0], 0) < ev[1]:
                d[ev[0]] = ev[1]

    def _waits(self, e, deps, own=None):
        ws = []
        for k, v in deps.items():
            if k == own and not self.same[e]:
                continue
            if self.waited[e].get(k, 0) >= v:
                continue
            self.waited[e][k] = v
            ws.append((self._sem(k), v))
        return ws

    def op(self, e, fn, r=(), w=()):
        psr = [t for t in r if isinstance(t, tuple) and t[0] == "ps"]
        if psr:
            w = list(w) + [t for t in psr if t not in w]
        own = ("c", e)
        deps = self._deps(r, w)
        ws = self._waits(e, deps, own)
        self.cnt[e] += 1
        ev = (own, self.cnt[e])
        sem = self.csem[e]

        def run(eng, ws=ws, fn=fn, sem=sem):
            for s_, v_ in ws:
                eng.wait_ge(s_, v_)
            fn(eng).then_inc(sem, 1)
        self.prog[e].append(run)
        self._commit(ev, r, w)

    def dma(self, q, fn, r=(), w=()):
        slot = self.dn[q] % self.kq
        self.dn[q] += 1
        deps = self._deps(r, w)
        k = ("d", q, slot)
        if self.duse[q][slot] > 0:
            v = 16 * self.duse[q][slot]
            if deps.get(k, 0) < v:
                deps[k] = v
        ws = self._waits(q, deps, None)
        self.duse[q][slot] += 1
        ev = (k, 16 * self.duse[q][slot])
        sem = self.dsem[q][slot]

        def run(eng, ws=ws, fn=fn, sem=sem):
            for s_, v_ in ws:
                eng.wait_ge(s_, v_)
            fn(eng).then_inc(sem, 16)
        self.prog[q].append(run)
        self._commit(ev, r, w)

    def barrier(self):
        evs = {}
        for e, c in self.cnt.items():
            if c:
                evs[("c", e)] = c
        for q in self.dsem:
            for i, u in enumerate(self.duse[q]):
                if u:
                    evs[("d", q, i)] = 16 * u
        for e in self.eng:
            ws = self._waits(e, dict(evs), None)

            def run(eng, ws=ws):
                for s_, v_ in ws:
                    eng.wait_ge(s_, v_)
            self.prog[e].append(run)

    def finish(self):
        evs = {}
        for q in self.dsem:
            for i, u in enumerate(self.duse[q]):
                if u:
                    evs[("d", q, i)] = 16 * u
        ws = self._waits("sp", evs, None)

        def run(eng, ws=ws):
            for s_, v_ in ws:
                eng.wait_ge(s_, v_)
        self.prog["sp"].append(run)

    def emit(self, block):
        for e, dec in (("sp", block.sync), ("act", block.scalar), ("dve", block.vector),
                       ("pool", block.gpsimd), ("pe", block.tensor)):
            def f(engobj, prog=self.prog[e]):
                for c in prog:
                    c(engobj)
            dec(f)


class Arena:
    def __init__(self, t, n):
        self.t, self.n, self.off = t, n, 0

    def reset(self):
        self.off = 0

    def alloc(self, *shape, dt=None):
        n = 1
        for s_ in shape:
            n *= s_
        assert self.off + n <= self.n, ("arena overflow", self.off, n, self.n)
        ap = self.t[:, self.off:self.off + n]
        self.off += n
        if len(shape) == 2:
            ap = ap.rearrange("p (a b) -> p a b", a=shape[0])
        elif len(shape) == 3:
            ap = ap.rearrange("p (a b c) -> p a b c", a=shape[0], b=shape[1])
        if dt is not None:
            ap = ap.bitcast(dt)
        return ap


def build_nc(stage=2, nt=NT):
    nc = bass.Bass("TRN2", target_bir_lowering=False)

    def D(name, shape, dt=F32):
        return nc.dram_tensor(name, shape, dt, kind="ExternalInput").ap()
    x_d = D("x", [S, DM])
    vecs_d = D("vecs", [NROWS, 128])
    w_in_d = D("w_in", [DM, 1792])
    wga_d = D("w_gate_a", [8, 64, 64])
    wgx_d = D("w_gate_x", [8, 64, 64])
    sinks_d = D("sinks", [1, 8])
    w_out_d = D("w_out", [DM, DM])
    gffn_d = D("ln_ffn_g", [1, DM])
    wq_d = D("w_query", [DM, 2048])
    skT_d = D("skT", [16, 128, 128])
    euv_d = D("expert_uv", [NEXP, 2 * DM])
    bias_d = D("bias_band", [128, 2048])
    mask_d = D("mask_band", [128, 2048])
    out_d = nc.dram_tensor("out", [S, DM], F32, kind="ExternalOutput").ap()

    with ExitStack() as es:
        def sb(name, shape, dt=F32):
            return es.enter_context(nc.sbuf_tensor(name, shape, dt))
        ident = sb("ident", [128, 128])
        pv = sb("pv", [128, 64])
        esink = sb("esink", [128, 8])
        ones_col = sb("ones_col", [128, 2])
        iota16 = sb("iota16", [128, 16])
        csp = sb("csp", [128, 4])
        ARENA_N = 48384 + 512 + 512 + 128 + 2048 + 128
        arena_t = sb("arena", [128, ARENA_N])
        ps = es.enter_context(nc.psum_tensor("ps", [128, 8, 512], F32))
        T = Trk(nc, es, kq=16)
        block = es.enter_context(nc.Block())
        AR = Arena(arena_t, ARENA_N)
        wa = AR.alloc(4, 128)
        wx = AR.alloc(4, 128)
        bones = AR.alloc(128)
        biasT = AR.alloc(2048)
        stag = AR.alloc(128)

        bank_ctr = [0]
        nbanks = [8]

        def nb():
            b = bank_ctr[0] % nbanks[0]
            bank_ctr[0] += 1
            return b

        def PB(b):
            return ("ps", b)

        def col(c):
            return pv[:, c:c + 1]

        T.op("pool", lambda g: g.memset(ident[:], 0.0), w=["ident"])
        T.op("pool", lambda g: g.affine_select(out=ident[:], in_=ident[:], pattern=[[-1, 128]],
                                               compare_op=ALU.not_equal, fill=1.0, base=0, channel_multiplier=1),
             r=["ident"], w=["ident"])
        T.op("pool", lambda g: g.memset(wa[:], 0.0), w=["wa"])
        T.op("pool", lambda g: g.memset(wx[:], 0.0), w=["wx"])
        T.op("pool", lambda g: g.memset(bones[:], 0.0), w=["bones"])
        T.op("pool", lambda g: g.memset(bones[0:64, 0:64], 1.0), r=["bones"], w=["bones"])
        T.op("pool", lambda g: g.memset(bones[64:128, 64:128], 1.0), r=["bones"], w=["bones"])
        T.op("pool", lambda g: g.memset(ones_col[:], 1.0), w=["ones_col"])
        T.op("pool", lambda g: g.iota(iota16[:], pattern=[[1, 16]], base=0, channel_multiplier=0,
                                      allow_small_or_imprecise_dtypes=True), w=["iota16"])
        T.op("pool", lambda g: g.memset(stag, 0.0), w=["stag"])
        T.dma("sp", lambda s: s.dma_start(out=stag[0:NROWS, :], in_=vecs_d), r=["stag"], w=["stag"])
        for n in range(8):
            c, p0 = n // 2, (n % 2) * 64
            T.dma("sp", lambda s, n=n, c=c, p0=p0: s.dma_start(out=wa[p0:p0 + 64, c, p0:p0 + 64], in_=wga_d[n]),
                  r=[], w=["wa"])
            T.dma("sp", lambda s, n=n, c=c, p0=p0: s.dma_start(out=wx[p0:p0 + 64, c, p0:p0 + 64], in_=wgx_d[n]),
                  r=[], w=["wx"])
        T.dma("sp", lambda s: s.dma_start(out=esink[:], in_=sinks_d[0, :].partition_broadcast(128)), w=["esink"])
        T.dma("sp", lambda s: s.dma_start(out=biasT[:], in_=bias_d), w=["biasT"])

        w_in_sb = AR.alloc(8, 1792)
        w_out_sb = AR.alloc(8, 1024)
        maskb = AR.alloc(2048)
        xt = [AR.alloc(1024) for _ in range(2)]
        junk = AR.alloc(1024)
        xs = AR.alloc(1024)
        hT = AR.alloc(8, 128)
        xbh = AR.alloc(4, 131)
        xc = AR.alloc(4, 128)
        rr = AR.alloc(4, 128)
        ig = AR.alloc(4, 128)
        aa = AR.alloc(4, 128)
        na2 = AR.alloc(512)
        sq = AR.alloc(512)
        gi = AR.alloc(512)
        bb = AR.alloc(4, 128)
        hh = [AR.alloc(4, 128) for _ in range(2)]
        gg = AR.alloc(512)
        yy = AR.alloc(4, 128)
        y2 = AR.alloc(4, 128)
        yg = AR.alloc(4, 128)
        q2 = AR.alloc(512)
        k2 = AR.alloc(128)
        sdq = AR.alloc(512)
        rq = AR.alloc(512)
        sdk = AR.alloc(128)
        rk = AR.alloc(128)
        qT = AR.alloc(4, 128)
        kT = [AR.alloc(128) for _ in range(2)]
        kTz = [[AR.alloc(128) for _ in range(2)] for _ in range(2)]
        gm = AR.alloc(2)
        qs = AR.alloc(512)
        ks = AR.alloc(128)
        vext = [AR.alloc(2, 65) for _ in range(2)]
        tmpS = [AR.alloc(512) for _ in range(4)]
        PT = [AR.alloc(512) for _ in range(4)]
        den = AR.alloc(8)
        rden = AR.alloc(8)
        yatt = AR.alloc(8, 64)
        yattT = AR.alloc(4, 128)
        ss1 = AR.alloc(2)
        sd1 = AR.alloc(2)
        rs1 = AR.alloc(2)
        ss2 = AR.alloc(2)
        sd2 = AR.alloc(2)
        rs2 = AR.alloc(2)
        tl = AR.alloc(4)
        print("[kernel] phase1 arena used", AR.off, "of", ARENA_N)

        w_in_v = w_in_d.rearrange("(c p) n -> p c n", p=128)
        for c in range(8):
            T.dma("sp", lambda s, c=c: s.dma_start(out=w_in_sb[:, c, :], in_=w_in_v[:, c, :]), w=[("w_in", c)])
        w_out_v = w_out_d.rearrange("(c p) n -> p c n", p=128)
        for c in range(8):
            T.dma("sp", lambda s, c=c: s.dma_start(out=w_out_sb[:, c, :], in_=w_out_v[:, c, :]), w=[("w_out", c)])
        T.dma("sp", lambda s: s.dma_start(out=maskb, in_=mask_d), w=["maskb"])

        b0 = nb()
        T.op("pe", lambda t: t.transpose(ps[:, b0, 0:128], stag, ident[:]),
             r=["stag", "ident"], w=[PB(b0)])
        T.op("dve", lambda v: v.tensor_copy(out=pv[:, 0:NROWS], in_=ps[:, b0, 0:NROWS]), r=[PB(b0)], w=["pv"])
        T.op("act", lambda a: a.activation(out=tl, in_=pv[:, R_L:R_L + 4], func=AF.Exp, scale=-1.0), r=["pv"], w=["tl"])
        T.op("act", lambda a: a.activation(out=tl, in_=tl, func=AF.Ln, bias=1.0), r=["tl"], w=["tl"])
        T.op("dve", lambda v: v.tensor_scalar(out=csp[:], in0=tl, scalar1=-8.0, scalar2=None, op0=ALU.mult),
             r=["tl"], w=["csp"])
        T.op("act", lambda a: a.activation(out=esink[:], in_=esink[:], func=AF.Exp), r=["esink"], w=["esink"])
        T.op("dve", lambda v: v.tensor_tensor(out=biasT[:], in0=biasT[:], in1=maskb, op=ALU.add),
             r=["biasT", "maskb"], w=["biasT"])
        T.op("pool", lambda g: g.memset(xbh[:, :, 0:3], 0.0), w=["xbh"])
        T.op("pool", lambda g: g.memset(gm, 0.0), w=["gm"])
        T.op("pool", lambda g: g.memset(gm[0:64, 0:1], 1.0), r=["gm"], w=["gm"])
        T.op("pool", lambda g: g.memset(gm[64:128, 1:2], 1.0), r=["gm"], w=["gm"])
        for i in range(2):
            T.op("pool", lambda g, i=i: g.memset(vext[i][:, :, 64:65], 1.0), w=[("vext", i)])

        BK_PT, bA, bB, bC, bD = 0, 2, 3, 4, 5
        back_banks = [6, 7, 1]
        bbc = [0]

        def nbk():
            b_ = back_banks[bbc[0] % 3]
            bbc[0] += 1
            return b_

        def front(t):
            X = xt[t % 2]
            XT = ("xt", t % 2)
            T.dma("sp", lambda s, t=t, X=X: s.dma_start(out=X, in_=x_d[t * 128:(t + 1) * 128, :]), w=[XT])
            T.op("dve", lambda v, X=X: v.scalar_tensor_tensor(out=junk, in0=X, scalar=1.0, in1=X, op0=ALU.mult,
                                                             op1=ALU.mult, accum_out=ss1[:, 0:1]),
                 r=[XT], w=["junk", "ss1"])
            T.op("act", lambda a: a.activation(out=sd1[:, 0:1], in_=ss1[:, 0:1], func=AF.Sqrt, scale=1.0 / DM, bias=EPS),
                 r=["ss1"], w=["sd1"])
            T.op("dve", lambda v: v.reciprocal(out=rs1[:, 0:1], in_=sd1[:, 0:1]), r=["sd1"], w=["rs1"])
            T.op("dve", lambda v, X=X: v.tensor_scalar(out=xs, in0=X, scalar1=rs1[:, 0:1], scalar2=None, op0=ALU.mult),
                 r=[XT, "rs1"], w=["xs"])
            for half in range(2):
                for c in range(half * 4, half * 4 + 4):
                    T.op("pe", lambda te, c=c: te.transpose(ps[:, BK_PT, (c % 4) * 128:(c % 4 + 1) * 128],
                                                             xs[:, c * 128:(c + 1) * 128], ident[:]),
                         r=["xs", "ident"], w=[PB(BK_PT)])
                for c in range(half * 4, half * 4 + 4):
                    T.op("act", lambda a, c=c: a.activation(out=hT[:, c, :], in_=ps[:, BK_PT, (c % 4) * 128:(c % 4 + 1) * 128],
                                                            func=AF.Identity, scale=col(R_GMIX + c)),
                         r=[PB(BK_PT), "pv"], w=[("hT", c)])
                yield
            for (bk, base) in ((bA, 0), (bB, 512), (bC, 1024)):
                for oc in range(4):
                    for kc in range(8):
                        T.op("pe", lambda te, bk=bk, base=base, oc=oc, kc=kc: te.matmul(
                            ps[:, bk, oc * 128:(oc + 1) * 128], lhsT=w_in_sb[:, kc, base + oc * 128: base + (oc + 1) * 128],
                            rhs=hT[:, kc, :], start=(kc == 0), stop=(kc == 7)),
                            r=[("w_in", kc), ("hT", kc)], w=[PB(bk)])
                    yield
            for kc in range(8):
                T.op("pe", lambda te, kc=kc: te.matmul(ps[:, bD, 0:128], lhsT=w_in_sb[:, kc, 1536:1664], rhs=hT[:, kc, :],
                                                       start=(kc == 0), stop=(kc == 7)),
                     r=[("w_in", kc), ("hT", kc)], w=[PB(bD)])
            yield
            for kc in range(8):
                T.op("pe", lambda te, kc=kc: te.matmul(ps[:, bD, 128:256], lhsT=hT[:, kc, :], rhs=w_in_sb[:, kc, 1664:1792],
                                                       start=(kc == 0), stop=(kc == 7)),
                     r=[("w_in", kc), ("hT", kc)], w=[PB(bD)])
            yield

        def back(t, nxt):
            X = xt[t % 2]
            XT = ("xt", t % 2)

            def step(n=2):
                for _ in range(n):
                    if next(nxt, "done") == "done":
                        break

            if t > 0:
                T.op("dve", lambda v: v.tensor_copy(out=xbh[:, :, 0:3], in_=xbh[:, :, 128:131]), r=["xbh"], w=["xbh"])
            T.op("act", lambda a: a.activation(out=xbh[:, :, 3:131], in_=ps[:, bA, :].rearrange("p (c t) -> p c t", c=4),
                                               func=AF.Identity), r=[PB(bA)], w=["xbh"])
            T.op("act", lambda a: a.activation(out=gg, in_=ps[:, bB, :], func=AF.Gelu_apprx_tanh), r=[PB(bB)], w=["gg"])
            T.op("act", lambda a: a.activation(out=q2, in_=ps[:, bC, :], func=AF.Square), r=[PB(bC)], w=["q2"])
            T.op("act", lambda a: a.activation(out=qs, in_=ps[:, bC, :], func=AF.Identity, scale=col(R_QG)), r=[PB(bC), "pv"], w=["qs"])
            T.op("act", lambda a: a.activation(out=k2, in_=ps[:, bD, 0:128], func=AF.Square), r=[PB(bD)], w=["k2"])
            T.op("act", lambda a: a.activation(out=ks, in_=ps[:, bD, 0:128], func=AF.Identity, scale=col(R_KG)), r=[PB(bD), "pv"], w=["ks"])
            VX = vext[t % 2]
            T.op("act", lambda a, VX=VX: a.activation(out=VX[:, :, 0:64], in_=ps[:, bD, 128:256].rearrange("p (g d) -> p g d", g=2),
                                                      func=AF.Identity), r=[PB(bD)], w=[("vext", t % 2)])
            step(2)
            for c in range(4):
                T.op("dve", lambda v, c=c: v.tensor_scalar(out=xc[:, c, :], in0=xbh[:, c, 3:131], scalar1=col(R_CW + 3 * 4 + c),
                                                           scalar2=col(R_CB + c), op0=ALU.mult, op1=ALU.add),
                     r=["xbh", "pv"], w=[("xc", c)])
                for tap in range(3):
                    T.op("dve", lambda v, c=c, tap=tap: v.scalar_tensor_tensor(
                        out=xc[:, c, :], in0=xbh[:, c, tap:tap + 128], scalar=col(R_CW + tap * 4 + c), in1=xc[:, c, :],
                        op0=ALU.mult, op1=ALU.add), r=["xbh", "pv", ("xc", c)], w=[("xc", c)])
            bE, bF = nbk(), nbk()
            for c in range(4):
                T.op("pe", lambda te, c=c: te.matmul(ps[:, bE, c * 128:(c + 1) * 128], lhsT=wa[:, c, :], rhs=xc[:, c, :],
                                                     start=True, stop=True), r=["wa", ("xc", c)], w=[PB(bE)])
            for c in range(4):
                T.op("pe", lambda te, c=c: te.matmul(ps[:, bF, c * 128:(c + 1) * 128], lhsT=wx[:, c, :], rhs=xc[:, c, :],
                                                     start=True, stop=True), r=["wx", ("xc", c)], w=[PB(bF)])
            step(2)
            for c in range(4):
                T.op("act", lambda a, c=c: a.activation(out=rr[:, c, :], in_=ps[:, bE, c * 128:(c + 1) * 128], func=AF.Sigmoid,
                                                        bias=col(R_BA + c)), r=[PB(bE), "pv"], w=[("rr", c)])
            for c in range(4):
                T.op("act", lambda a, c=c: a.activation(out=ig[:, c, :], in_=ps[:, bF, c * 128:(c + 1) * 128], func=AF.Sigmoid,
                                                        bias=col(R_BX + c)), r=[PB(bF), "pv"], w=[("ig", c)])
            for c in range(4):
                T.op("act", lambda a, c=c: a.activation(out=aa[:, c, :], in_=rr[:, c, :], func=AF.Exp, scale=csp[:, c:c + 1]),
                     r=[("rr", c), "csp"], w=[("aa", c)])
            aaf = aa.rearrange("p c t -> p (c t)")
            T.op("dve", lambda v: v.scalar_tensor_tensor(out=na2, in0=aaf, scalar=-1.0, in1=aaf, op0=ALU.mult, op1=ALU.mult),
                 r=[("aa", c) for c in range(4)], w=["na2"])
            T.op("act", lambda a: a.activation(out=sq, in_=na2, func=AF.Sqrt, bias=1.0), r=["na2"], w=["sq"])
            T.op("dve", lambda v: v.tensor_tensor(out=gi, in0=ig.rearrange("p c t -> p (c t)"),
                                                  in1=xc.rearrange("p c t -> p (c t)"), op=ALU.mult),
                 r=[("ig", c) for c in range(4)] + [("xc", c) for c in range(4)], w=["gi"])
            T.op("dve", lambda v: v.tensor_tensor(out=bb.rearrange("p c t -> p (c t)"), in0=sq, in1=gi, op=ALU.mult),
                 r=["sq", "gi"], w=["bb"])
            H = hh[t % 2]
            Hp = hh[(t - 1) % 2]
            for c in range(4):
                init = Hp[:, c, 127:128] if t > 0 else 0.0
                T.op("dve", lambda v, c=c, init=init, H=H: v.tensor_tensor_scan(out=H[:, c, :], data0=aa[:, c, :], data1=bb[:, c, :],
                                                                                initial=init, op0=ALU.mult, op1=ALU.add),
                     r=[("aa", c), "bb", ("hh", (t - 1) % 2)], w=[("hh", t % 2)])
            T.op("dve", lambda v, H=H: v.tensor_tensor(out=yy.rearrange("p c t -> p (c t)"), in0=H.rearrange("p c t -> p (c t)"),
                                                       in1=gg, op=ALU.mult), r=[("hh", t % 2), "gg"], w=["yy"])
            T.op("pool", lambda g: g.tensor_tensor(out=y2.rearrange("p c t -> p (c t)"), in0=yy.rearrange("p c t -> p (c t)"),
                                                   in1=yy.rearrange("p c t -> p (c t)"), op=ALU.mult), r=["yy"], w=["y2"])
            for c in range(4):
                T.op("dve", lambda v, c=c: v.tensor_scalar(out=yg[:, c, :], in0=yy[:, c, :], scalar1=col(R_GLRU + c), scalar2=None,
                                                           op0=ALU.mult), r=["yy", "pv"], w=[("yg", c)])

            bG, bH = nbk(), nbk()
            T.op("pe", lambda te: te.matmul(ps[:, bG, :], lhsT=bones[:], rhs=q2, start=True, stop=True),
                 r=["bones", "q2"], w=[PB(bG)])
            T.op("pe", lambda te: te.matmul(ps[:, bH, 0:128], lhsT=bones[:], rhs=k2, start=True, stop=True),
                 r=["bones", "k2"], w=[PB(bH)])
            step(2)
            T.op("act", lambda a: a.activation(out=sdq, in_=ps[:, bG, :], func=AF.Sqrt, scale=1.0 / 64, bias=EPS),
                 r=[PB(bG)], w=["sdq"])
            T.op("act", lambda a: a.activation(out=sdk, in_=ps[:, bH, 0:128], func=AF.Sqrt, scale=1.0 / 64, bias=EPS),
                 r=[PB(bH)], w=["sdk"])
            T.op("dve", lambda v: v.reciprocal(out=rq, in_=sdq), r=["sdq"], w=["rq"])
            T.op("dve", lambda v: v.reciprocal(out=rk, in_=sdk), r=["sdk"], w=["rk"])
            T.op("dve", lambda v: v.tensor_tensor(out=qT.rearrange("p c t -> p (c t)"), in0=qs, in1=rq, op=ALU.mult),
                 r=["qs", "rq"], w=["qT"])
            KT = kT[t % 2]
            T.op("dve", lambda v, KT=KT: v.tensor_tensor(out=KT, in0=ks, in1=rk, op=ALU.mult),
                 r=["ks", "rk"], w=[("kT", t % 2)])
            for g in range(2):
                T.op("pool", lambda ge, g=g, KT=KT: ge.tensor_scalar(out=kTz[t % 2][g], in0=KT, scalar1=gm[:, g:g + 1], scalar2=1.0,
                                                                    op0=ALU.mult, op1=ALU.mult),
                     r=[("kT", t % 2), "gm"], w=[("kTz", t % 2, g)])
            ktiles = ([t - 1] if t > 0 else []) + [t]
            for g in range(2):
                for ci, kt in enumerate(ktiles):
                    b = nbk()
                    cc = 1 if kt == t else 0
                    i4 = g * 2 + ci
                    boff = (g * 2 + cc) * 512
                    T.op("pe", lambda te, g=g, kt=kt, b=b: te.matmul(
                        ps[:, b, :], lhsT=kTz[kt % 2][g], rhs=qT,
                        start=True, stop=True), r=[("kTz", kt % 2, g), "qT"], w=[PB(b)])
                    T.op("dve", lambda v, b=b, i4=i4, boff=boff: v.scalar_tensor_tensor(
                        out=tmpS[i4], in0=ps[:, b, :], scalar=SCALE, in1=biasT[:, boff:boff + 512], op0=ALU.mult, op1=ALU.add),
                        r=[PB(b), "biasT"], w=[("tmpS", i4)])
                    T.op("act", lambda a, i4=i4: a.activation(out=PT[i4], in_=tmpS[i4], func=AF.Exp),
                         r=[("tmpS", i4)], w=[("PT", i4)])
            step(2)
            bO = [nbk(), nbk()]
            for g in range(2):
                for j in range(4):
                    for ci, kt in enumerate(ktiles):
                        i4 = g * 2 + ci
                        T.op("pe", lambda te, g=g, j=j, ci=ci, kt=kt, i4=i4: te.matmul(
                            ps[:, bO[g], j * 128:j * 128 + 65], lhsT=PT[i4][:, j * 128:(j + 1) * 128], rhs=vext[kt % 2][:, g, :],
                            start=(ci == 0), stop=(ci == len(ktiles) - 1)),
                            r=[("PT", i4), ("vext", kt % 2)], w=[PB(bO[g])])
            bT_ss = nbk()
            for c in range(4):
                T.op("pe", lambda te, c=c: te.matmul(ps[:, bT_ss, 300:301], lhsT=y2[:, c, :], rhs=ones_col[:, 0:1],
                                                     start=(c == 0), stop=(c == 3)), r=["y2", "ones_col"], w=[PB(bT_ss)])
            T.op("act", lambda a: a.activation(out=ss2[:, 0:1], in_=ps[:, bT_ss, 300:301], func=AF.Identity), r=[PB(bT_ss)], w=["ss2a"])
            step(3)
            for g in range(2):
                ov = ps[:, bO[g], :].rearrange("p (j e) -> p j e", j=4)
                T.op("dve", lambda v, g=g, ov=ov: v.tensor_tensor(out=den[:, g * 4:(g + 1) * 4], in0=ov[:, :, 64],
                                                                  in1=esink[:, g * 4:(g + 1) * 4], op=ALU.add),
                     r=[PB(bO[g]), "esink"], w=[("den", g)])
            T.op("dve", lambda v: v.reciprocal(out=rden, in_=den), r=[("den", 0), ("den", 1)], w=["rden"])
            for g in range(2):
                ov = ps[:, bO[g], :].rearrange("p (j e) -> p j e", j=4)
                T.op("dve", lambda v, g=g, ov=ov: v.tensor_tensor(
                    out=yatt[:, g * 4:(g + 1) * 4, :], in0=ov[:, :, 0:64],
                    in1=rden[:, g * 4:(g + 1) * 4].unsqueeze(2).broadcast_to([128, 4, 64]), op=ALU.mult),
                    r=[PB(bO[g]), "rden"], w=[("yatt", g)])
            yaf = yatt.rearrange("p h d -> p (h d)")
            T.op("dve", lambda v: v.scalar_tensor_tensor(out=junk[:, 0:512], in0=yaf, scalar=1.0, in1=yaf, op0=ALU.mult,
                                                         op1=ALU.mult, accum_out=ss2[:, 1:2]),
                 r=[("yatt", 0), ("yatt", 1)], w=["junk", "ss2b"])
            bT = nbk()
            for c in range(4):
                T.op("pe", lambda te, c=c: te.transpose(ps[:, bT, c * 128:(c + 1) * 128], yaf[:, c * 128:(c + 1) * 128], ident[:]),
                     r=[("yatt", 0), ("yatt", 1), "ident"], w=[PB(bT)])
            step(2)
            for c in range(4):
                T.op("act", lambda a, c=c: a.activation(out=yattT[:, c, :], in_=ps[:, bT, c * 128:(c + 1) * 128], func=AF.Identity,
                                                        scale=col(R_GATT + c)), r=[PB(bT), "pv"], w=[("yattT", c)])
            T.op("act", lambda a: a.activation(out=sd2, in_=ss2, func=AF.Sqrt, scale=1.0 / 512, bias=EPS),
                 r=["ss2a", "ss2b"], w=["sd2"])
            T.op("dve", lambda v: v.reciprocal(out=rs2, in_=sd2), r=["sd2"], w=["rs2"])
            bL = [nbk(), nbk()]
            for hf in range(2):
                for c in range(4):
                    T.op("pe", lambda te, hf=hf, c=c: te.matmul(ps[:, bL[hf], :], lhsT=yg[:, c, :],
                                                                rhs=w_out_sb[:, c, hf * 512:(hf + 1) * 512],
                                                                start=(c == 0), stop=(c == 3)),
                         r=[("yg", c), ("w_out", c)], w=[PB(bL[hf])])
            for hf in range(2):
                T.op("dve", lambda v, hf=hf, X=X: v.scalar_tensor_tensor(
                    out=X[:, hf * 512:(hf + 1) * 512], in0=ps[:, bL[hf], :], scalar=rs2[:, 0:1], in1=X[:, hf * 512:(hf + 1) * 512],
                    op0=ALU.mult, op1=ALU.add), r=[PB(bL[hf]), "rs2", XT], w=[XT])
            step(2)
            bM = [nbk(), nbk()]
            for hf in range(2):
                for c in range(4):
                    T.op("pe", lambda te, hf=hf, c=c: te.matmul(ps[:, bM[hf], :], lhsT=yattT[:, c, :],
                                                                rhs=w_out_sb[:, 4 + c, hf * 512:(hf + 1) * 512],
                                                                start=(c == 0), stop=(c == 3)),
                         r=[("yattT", c), ("w_out", 4 + c)], w=[PB(bM[hf])])
            for hf in range(2):
                T.op("dve", lambda v, hf=hf, X=X: v.scalar_tensor_tensor(
                    out=X[:, hf * 512:(hf + 1) * 512], in0=ps[:, bM[hf], :], scalar=rs2[:, 1:2], in1=X[:, hf * 512:(hf + 1) * 512],
                    op0=ALU.mult, op1=ALU.add), r=[PB(bM[hf]), "rs2", XT], w=[XT])
            for _ in nxt:
                pass
            T.dma("sp", lambda s, t=t, X=X: s.dma_start(out=out_d[t * 128:(t + 1) * 128, :], in_=X), r=[XT], w=[("out", t)])

        for _ in front(0):
            pass
        for t in range(nt):
            back(t, front(t + 1) if t + 1 < nt else iter(()))

        if stage >= 2:
            T.barrier()
            AR.reset()
            nbanks[0] = 6 if int(os.environ.get("PE_EVERY", "0")) > 0 else 8
            NDG = 4
            PE_EVERY = int(os.environ.get("PE_EVERY", "0"))

            def alloc_rest(AR):
                g = {}
                g['wq_sb'] = AR.alloc(8, 2048)
                g['skT_sb'] = AR.alloc(16, 128)
                g['gffn_b'] = AR.alloc(1024)
                g['xn'] = [AR.alloc(1024) for _ in range(2)]
                g['xn2'] = [AR.alloc(1024) for _ in range(2)]
                g['xn2T'] = AR.alloc(8, 128)
                g['qpT'] = AR.alloc(16, 128)
                g['s_all'] = AR.alloc(16, 128)
                g['swk'] = AR.alloc(16, 128)
                g['cand'] = AR.alloc(8, 256)
                g['stop'] = AR.alloc(16, 16)
                g['sidx'] = AR.alloc(16, 16, dt=U32)
                g['sidxf'] = AR.alloc(16, 16)
                g['best'] = AR.alloc(8, 16)
                g['pos'] = AR.alloc(8, 16, dt=U32)
                g['pa'] = AR.alloc(128, dt=U32)
                g['pb_'] = AR.alloc(128, dt=U32)
                g['paf'] = AR.alloc(128)
                g['pbf'] = AR.alloc(128)
                g['i1f'] = AR.alloc(128)
                g['i2f'] = AR.alloc(128)
                g['idxf'] = AR.alloc(128)
                g['idx_i'] = [AR.alloc(128, dt=I32) for _ in range(2)]
                g['bsub'] = AR.alloc(8, 16)
                g['eb'] = AR.alloc(8, 16)
                g['gw'] = AR.alloc(8, 16)
                g['AR_extra_gw'] = AR.alloc(8, 16)
                g['dgs'] = [AR.alloc(128) for _ in range(NDG)]
                g['se'] = AR.alloc(8)
                g['rse'] = AR.alloc(8)
                g['actv'] = AR.alloc(128)
                g['gact'] = AR.alloc(128)
                g['wv'] = [AR.alloc(128) for _ in range(2)]
                g['ss3'] = AR.alloc(2)
                g['sd3'] = AR.alloc(2)
                g['rs3'] = AR.alloc(2)
                return g
            class _Dry(Arena):
                def alloc(self, *shape, dt=None):
                    n = 1
                    for s_ in shape:
                        n *= s_
                    self.off += n
                    return None
            dry = _Dry(None, 10 ** 9)
            alloc_rest(dry)
            NS = min(int(os.environ.get("NSLAB", "16")), (ARENA_N - dry.off) // 2048)
            assert NS >= 4, NS
            G = alloc_rest(AR)
            slabs = [AR.alloc(2048) for _ in range(NS)]
            junk_ctr = [0]
            dg_ctr = [0]

            def nj():
                i = junk_ctr[0] % 2
                junk_ctr[0] += 1
                return junk2r[i], ("junk2", i)
            wq_sb = G['wq_sb']
            skT_sb = G['skT_sb']
            gffn_b = G['gffn_b']
            xn = G['xn']
            xn2 = G['xn2']
            xn2T = G['xn2T']
            qpT = G['qpT']
            s_all = G['s_all']
            swk = G['swk']
            cand = G['cand']
            stop = G['stop']
            sidx = G['sidx']
            sidxf = G['sidxf']
            best = G['best']
            pos = G['pos']
            pa = G['pa']
            pb_ = G['pb_']
            paf = G['paf']
            pbf = G['pbf']
            i1f = G['i1f']
            i2f = G['i2f']
            idxf = G['idxf']
            idx_i = G['idx_i']
            bsub = G['bsub']
            eb = G['eb']
            gw = G['gw']
            AR_extra_gw = G['AR_extra_gw']
            dgs = G['dgs']
            se = G['se']
            rse = G['rse']
            actv = G['actv']
            gact = G['gact']
            wv = G['wv']
            ss3 = G['ss3']
            sd3 = G['sd3']
            rs3 = G['rs3']
            print("[kernel] phase2 arena used", AR.off, "of", ARENA_N, "slabs", NS)
            cwk = qpT.rearrange("p a b -> p (a b)").rearrange("p (h n) -> p h n", h=8)
            eqa = s_all.rearrange("p a b -> p (a b)").rearrange("p (k a) -> p k a", a=16)
            eqb = swk.rearrange("p a b -> p (a b)").rearrange("p (k a) -> p k a", a=16)

            wq_v = wq_d.rearrange("(c p) n -> p c n", p=128)
            for c in range(8):
                T.dma("sp", lambda s, c=c: s.dma_start(out=wq_sb[:, c, :], in_=wq_v[:, c, :]), w=[("wq", c)])
            T.dma("sp", lambda s: s.dma_start(out=skT_sb, in_=skT_d.rearrange("h d n -> d h n")), w=["skT"])
            T.dma("sp", lambda s: s.dma_start(out=gffn_b, in_=gffn_d[0, :].partition_broadcast(128)), w=["gffn"])
            slab_ctr = [0]
            for i in range(2):
                T.op("pool", lambda g, i=i: g.memset(idx_i[i], 0), w=[("idx", i)])

            gw2 = [gw, AR_extra_gw]

            def prep(t):
                XN = xn[t % 2]
                XNT = ("xn", t % 2)
                X2 = xn2[t % 2]
                X2T = ("xn2", t % 2)
                GW = gw2[t % 2]
                GWT = ("gw", t % 2)
                T.dma("sp", lambda s, t=t, XN=XN: s.dma_start(out=XN, in_=out_d[t * 128:(t + 1) * 128, :]),
                      r=[("out", t)], w=[XNT])
                T.op("dve", lambda v, XN=XN, X2=X2: v.scalar_tensor_tensor(out=X2, in0=XN, scalar=1.0, in1=XN, op0=ALU.mult,
                                                                          op1=ALU.mult, accum_out=ss3[:, 0:1]),
                     r=[XNT], w=["ss3", X2T])
                yield
                T.op("act", lambda a: a.activation(out=sd3[:, 0:1], in_=ss3[:, 0:1], func=AF.Sqrt, scale=1.0 / DM, bias=EPS),
                     r=["ss3"], w=["sd3"])
                T.op("dve", lambda v: v.reciprocal(out=rs3[:, 0:1], in_=sd3[:, 0:1]), r=["sd3"], w=["rs3"])
                yield
                T.op("dve", lambda v, XN=XN, X2=X2: v.scalar_tensor_tensor(out=X2, in0=XN, scalar=rs3[:, 0:1], in1=gffn_b,
                                                                          op0=ALU.mult, op1=ALU.mult),
                     r=[XNT, "rs3", "gffn"], w=[X2T])
                yield
                pt = [nb(), nb()]
                for c in range(8):
                    T.op("pe", lambda te, c=c, X2=X2: te.transpose(ps[:, pt[c // 4], (c % 4) * 128:(c % 4 + 1) * 128],
                                                                  X2[:, c * 128:(c + 1) * 128], ident[:]),
                         r=[X2T, "ident"], w=[PB(pt[c // 4])])
                for i in range(2):
                    T.op("act", lambda a, i=i: a.activation(out=xn2T[:, i * 4:(i + 1) * 4, :],
                                                            in_=ps[:, pt[i], :].rearrange("p (c t) -> p c t", c=4), func=AF.Identity),
                         r=[PB(pt[i])], w=[("xn2T", i)])
                bq = [nb(), nb(), nb(), nb()]
                for hc in range(16):
                    for kc in range(8):
                        T.op("pe", lambda te, hc=hc, kc=kc: te.matmul(
                            ps[:, bq[hc // 4], (hc % 4) * 128:(hc % 4 + 1) * 128], lhsT=wq_sb[:, kc, hc * 128:(hc + 1) * 128],
                            rhs=xn2T[:, kc, :], start=(kc == 0), stop=(kc == 7)),
                            r=[("wq", kc), ("xn2T", kc // 4)], w=[PB(bq[hc // 4])])
                for i in range(4):
                    T.op("act", lambda a, i=i: a.activation(out=qpT[:, i * 4:(i + 1) * 4, :],
                                                            in_=ps[:, bq[i], :].rearrange("p (c t) -> p c t", c=4), func=AF.Identity),
                         r=[PB(bq[i])], w=[("qpT", i)])
                bs_ = [nb(), nb(), nb(), nb()]
                for hc in range(16):
                    T.op("pe", lambda te, hc=hc: te.matmul(ps[:, bs_[hc // 4], (hc % 4) * 128:(hc % 4 + 1) * 128],
                                                           lhsT=qpT[:, hc, :], rhs=skT_sb[:, hc, :], start=True, stop=True),
                         r=[("qpT", hc // 4), "skT"], w=[PB(bs_[hc // 4])])
                for i in range(4):
                    T.op("act", lambda a, i=i: a.activation(out=s_all[:, i * 4:(i + 1) * 4, :],
                                                            in_=ps[:, bs_[i], :].rearrange("p (c t) -> p c t", c=4), func=AF.Identity),
                         r=[PB(bs_[i])], w=[("s_all", i)])
                for hc in range(16):
                    SA = ("s_all", hc // 4)
                    SW = ("swk", hc)
                    ST0, ST1 = ("stop", hc, 0), ("stop", hc, 1)
                    T.op("dve", lambda v, hc=hc: v.max(out=stop[:, hc, 0:8], in_=s_all[:, hc, :]), r=[SA], w=[ST0])
                    yield
                    T.op("dve", lambda v, hc=hc: v.max_index(out=sidx[:, hc, 0:8], in_max=stop[:, hc, 0:8], in_values=s_all[:, hc, :]),
                         r=[SA, ST0], w=[("sidx", hc, 0)])
                    yield
                    T.op("dve", lambda v, hc=hc: v.match_replace(out=swk[:, hc, :], in_to_replace=stop[:, hc, 0:8],
                                                                 in_values=s_all[:, hc, :], imm_value=-1e30),
                         r=[SA, ST0], w=[SW])
                    yield
                    T.op("dve", lambda v, hc=hc: v.max(out=stop[:, hc, 8:16], in_=swk[:, hc, :]), r=[SW], w=[ST1])
                    yield
                    T.op("dve", lambda v, hc=hc: v.max_index(out=sidx[:, hc, 8:16], in_max=stop[:, hc, 8:16], in_values=swk[:, hc, :]),
                         r=[SW, ST1], w=[("sidx", hc, 1)])
                    yield
                STALL = [("stop", hc, i) for hc in range(16) for i in range(2)]
                SIALL = [("sidx", hc, i) for hc in range(16) for i in range(2)]
                T.op("dve", lambda v: v.tensor_copy(out=sidxf, in_=sidx), r=SIALL, w=["sidxf"])
                yield
                st4 = stop.rearrange("p (h c) k -> p h c k", c=2)
                sf4 = sidxf.rearrange("p (h c) k -> p h c k", c=2)
                T.op("dve", lambda v: v.tensor_tensor(out=cand.rearrange("p h (a b) -> p h a b", a=16),
                                                      in0=st4[:, :, 0, :].unsqueeze(3).broadcast_to([128, 8, 16, 16]),
                                                      in1=st4[:, :, 1, :].unsqueeze(2).broadcast_to([128, 8, 16, 16]), op=ALU.add),
                     r=STALL, w=["cand"])
                yield
                for h in range(8):
                    CWh = ("cwk", h)
                    B0, B1 = ("best", h, 0), ("best", h, 1)
                    T.op("dve", lambda v, h=h: v.max(out=best[:, h, 0:8], in_=cand[:, h, :]), r=["cand"], w=[B0])
                    yield
                    T.op("dve", lambda v, h=h: v.max_index(out=pos[:, h, 0:8], in_max=best[:, h, 0:8], in_values=cand[:, h, :]),
                         r=["cand", B0], w=[("pos", h, 0)])
                    yield
                    T.op("dve", lambda v, h=h: v.match_replace(out=cwk[:, h, :], in_to_replace=best[:, h, 0:8],
                                                               in_values=cand[:, h, :], imm_value=-1e30),
                         r=["cand", B0] + [("qpT", i) for i in range(4)], w=[CWh])
                    yield
                    T.op("dve", lambda v, h=h: v.max(out=best[:, h, 8:16], in_=cwk[:, h, :]), r=[CWh], w=[B1])
                    yield
                    T.op("dve", lambda v, h=h: v.max_index(out=pos[:, h, 8:16], in_max=best[:, h, 8:16], in_values=cwk[:, h, :]),
                         r=[CWh, B1], w=[("pos", h, 1)])
                    yield
                BALL = [("best", h, i) for h in range(8) for i in range(2)]
                PALL = [("pos", h, i) for h in range(8) for i in range(2)]
                CWALL = [("cwk", h) for h in range(8)]
                posf = pos.rearrange("p h k -> p (h k)")
                T.op("dve", lambda v: v.tensor_single_scalar(out=pa, in_=posf, scalar=4, op=ALU.logical_shift_right),
                     r=PALL, w=["pa"])
                T.op("dve", lambda v: v.tensor_single_scalar(out=pb_, in_=posf, scalar=15, op=ALU.bitwise_and),
                     r=PALL, w=["pb"])
                yield
                T.op("dve", lambda v: v.tensor_copy(out=paf, in_=pa), r=["pa"], w=["paf"])
                T.op("dve", lambda v: v.tensor_copy(out=pbf, in_=pb_), r=["pb"], w=["pbf"])
                yield
                EA = [("s_all", i) for i in range(4)]
                EB = [("swk", hc) for hc in range(16)]
                for (pf, eq, ET, ci, of, tok) in ((paf, eqa, EA, 0, i1f, "i1f"), (pbf, eqb, EB, 1, i2f, "i2f")):
                    T.op("dve", lambda v, pf=pf, eq=eq: v.tensor_tensor(out=eq, in0=pf.unsqueeze(2).broadcast_to([128, 128, 16]),
                                                                        in1=iota16[:].unsqueeze(1).broadcast_to([128, 128, 16]),
                                                                        op=ALU.is_equal),
                         r=["paf", "pbf", "iota16"], w=ET)
                    yield
                    eq4 = eq.rearrange("p (h k) a -> p h k a", h=8)
                    T.op("dve", lambda v, eq4=eq4, ci=ci: v.tensor_tensor(out=eq4, in0=eq4,
                                                                          in1=sf4[:, :, ci, :].unsqueeze(2).broadcast_to([128, 8, 16, 16]),
                                                                          op=ALU.mult),
                         r=ET + ["sidxf"], w=ET)
                    yield
                    T.op("dve", lambda v, eq=eq, of=of: v.tensor_reduce(out=of, in_=eq, axis=AX.X, op=ALU.add), r=ET, w=[tok])
                    yield
                T.op("dve", lambda v: v.scalar_tensor_tensor(out=idxf, in0=i1f, scalar=128.0, in1=i2f, op0=ALU.mult, op1=ALU.add),
                     r=["i1f", "i2f"], w=["idxf"])
                T.op("dve", lambda v: v.tensor_scalar(out=idxf, in0=idxf, scalar1=0.0, scalar2=float(NEXP - 1), op0=ALU.max, op1=ALU.min),
                     r=["idxf"], w=["idxf"])
                IDX = idx_i[t % 2]
                IDXT = ("idx", t % 2)
                T.op("dve", lambda v, IDX=IDX: v.tensor_copy(out=IDX, in_=idxf), r=["idxf"], w=[IDXT])
                yield
                T.op("dve", lambda v: v.tensor_tensor(out=bsub, in0=best, in1=best[:, :, 0:1].broadcast_to([128, 8, 16]),
                                                      op=ALU.subtract), r=BALL, w=["bsub"])
                T.op("act", lambda a: a.activation(out=eb, in_=bsub, func=AF.Exp), r=["bsub"], w=["eb"])
                T.op("dve", lambda v: v.tensor_reduce(out=se, in_=eb, axis=AX.X, op=ALU.add), r=["eb"], w=["se"])
                yield
                T.op("dve", lambda v: v.reciprocal(out=rse, in_=se), r=["se"], w=["rse"])
                T.op("dve", lambda v, GW=GW: v.tensor_tensor(out=GW, in0=eb, in1=rse.unsqueeze(2).broadcast_to([128, 8, 16]), op=ALU.mult),
                     r=["eb", "rse"], w=[GWT])
                yield

            def gather_phase(t, nxt, per_slot):
                XN = xn[t % 2]
                XNT = ("xn", t % 2)
                X2 = xn2[t % 2]
                X2T = ("xn2", t % 2)
                IDX = idx_i[t % 2]
                IDXT = ("idx", t % 2)
                GW = gw2[t % 2].rearrange("p h k -> p (h k)")
                GWT = ("gw", t % 2)
                WV = wv[t % 2]
                credit = [0.0]

                def step():
                    credit[0] += per_slot
                    while credit[0] >= 1.0:
                        credit[0] -= 1.0
                        if next(nxt, "done") == "done":
                            break
                LAG = 2
                used = {}
                for hk in range(128 + LAG):
                    if hk < 128:
                        si = slab_ctr[0] % NS
                        slab_ctr[0] += 1
                        SL = slabs[si]
                        SLT = ("slab", si)
                        used[hk] = (SL, SLT)
                        T.dma("pool", lambda g, hk=hk, SL=SL, IDX=IDX: g.indirect_dma_start(
                            out=SL, out_offset=None, in_=euv_d, in_offset=bass.IndirectOffsetOnAxis(ap=IDX[:, hk:hk + 1], axis=0)),
                            r=[IDXT], w=[SLT])
                        T.op("dve", lambda v, hk=hk, SL=SL, X2=X2: v.scalar_tensor_tensor(
                            out=SL[:, 0:1024], in0=SL[:, 0:1024], scalar=1.0, in1=X2, op0=ALU.mult, op1=ALU.mult,
                            accum_out=actv[:, hk:hk + 1]), r=[SLT, X2T], w=[("actv", hk), SLT])
                        T.op("act", lambda a, hk=hk: a.activation(out=gact[:, hk:hk + 1], in_=actv[:, hk:hk + 1], func=AF.Gelu_apprx_tanh),
                             r=[("actv", hk)], w=[("gact", hk)])
                        T.op("act", lambda a, hk=hk, WV=WV, GW=GW: a.activation(out=WV[:, hk:hk + 1], in_=gact[:, hk:hk + 1], func=AF.Identity,
                                                                            scale=GW[:, hk:hk + 1]),
                             r=[("gact", hk), GWT], w=[("wv", t % 2, hk)])
                    k2 = hk - LAG
                    if k2 >= 0:
                        SL2, SLT2 = used.pop(k2)
                        T.op("dve", lambda v, k2=k2, SL2=SL2, XN=XN, WV=WV: v.scalar_tensor_tensor(
                            out=XN, in0=SL2[:, 1024:2048], scalar=WV[:, k2:k2 + 1], in1=XN, op0=ALU.mult, op1=ALU.add),
                            r=[SLT2, ("wv", t % 2, k2), XNT], w=[XNT])
                    step()
                for _ in nxt:
                    pass
                T.dma("sp", lambda s, t=t, XN=XN: s.dma_start(out=out_d[t * 128:(t + 1) * 128, :], in_=XN),
                      r=[XNT], w=[("out", t)])

            for _ in prep(0):
                pass
            for t in range(nt):
                nxt = prep(t + 1) if t + 1 < nt else iter(())
                gather_phase(t, nxt, float(os.environ.get("PREP_RATE", "1.25")))

        T.finish()
        T.emit(block)
        build_nc.dbg = {"yy": yy.offset, "yatt": yatt.offset, "qT": qT.offset, "xbh": xbh.offset, "gg": gg.offset, "hT": hT.offset}
    return nc


def _t5_bucket(rel):
    n = np.maximum(rel, 0)
    max_exact = 16
    nf = np.maximum(n, 1).astype(np.float32)
    large = max_exact + (np.log(nf / np.float32(max_exact)) / np.float32(np.log(128 / max_exact))
                         * np.float32(32 - max_exact)).astype(np.int32)
    large = np.minimum(large, 31)
    return np.where(n < max_exact, n, large)


def _prep_shared(inp):
    f = lambda a: np.ascontiguousarray(np.asarray(a, dtype=np.float32))
    w_in = f(inp["w_in"])[0]
    qcols = np.arange(1024, 1536).reshape(2, 4, 64)
    perm = np.concatenate([np.arange(0, 1024), qcols.transpose(1, 0, 2).reshape(-1), np.arange(1536, 1792)])
    w_in_p = np.ascontiguousarray(w_in[:, perm])
    rows = []
    cw = f(inp["conv_w"])[0]
    rows.append(cw.reshape(4, 4, 128).reshape(16, 128))
    for k in ("conv_b", "b_gate_a", "b_gate_x", "lru_L", "lru_out_g", "attn_out_g"):
        rows.append(f(inp[k]).reshape(4, 128))
    rows.append(f(inp["ln_mix_g"]).reshape(8, 128))
    rows.append(np.tile(f(inp["q_norm_g"]).reshape(1, 64), (1, 2)))
    rows.append(np.tile(f(inp["k_norm_g"]).reshape(1, 64), (1, 2)))
    vecs = np.ascontiguousarray(np.concatenate(rows, axis=0))
    assert vecs.shape == (NROWS, 128)
    rel_bias = f(inp["rel_bias"])
    j = np.arange(128)[:, None]
    i = np.arange(128)[None, :]
    rel_cur = i - j
    rel_prev = 128 + i - j
    bias_band = np.zeros((128, 2, 2, 4, 128), np.float32)
    mask_band = np.zeros((128, 2, 2, 4, 128), np.float32)
    for c, rel, valid in ((0, rel_prev, rel_prev < 128), (1, rel_cur, rel_cur >= 0)):
        bk = _t5_bucket(rel)
        for g in range(2):
            for jj in range(4):
                bias_band[:, g, c, jj, :] = rel_bias[bk, g * 4 + jj]
                mask_band[:, g, c, jj, :] = np.where(valid, 0.0, MASKV)
    sk = f(inp["sub_keys"])[0]
    skT = np.ascontiguousarray(sk.reshape(16, 128, 128).transpose(0, 2, 1))
    return {
        "vecs": vecs, "w_in": w_in_p,
        "w_gate_a": f(inp["w_gate_a"])[0], "w_gate_x": f(inp["w_gate_x"])[0],
        "sinks": f(inp["sinks"]).reshape(1, 8), "w_out": f(inp["w_out"])[0],
        "ln_ffn_g": f(inp["ln_ffn_g"]).reshape(1, DM), "w_query": f(inp["w_query"])[0],
        "skT": skT, "expert_uv": np.ascontiguousarray(np.concatenate([f(inp["expert_u"])[0], f(inp["expert_v"])[0]], axis=1)),
        "bias_band": bias_band.reshape(128, 2048), "mask_band": mask_band.reshape(128, 2048),
    }


def kernel(**inputs):
    stage = int(os.environ.get("KSTAGE", "2"))
    shared = _prep_shared(inputs)
    x = np.ascontiguousarray(np.asarray(inputs["x"], dtype=np.float32))
    nc = build_nc(stage)
    in_maps = []
    for b in range(8):
        m = dict(shared)
        m["x"] = x[b]
        in_maps.append(m)
    res = run_bass_kernel_spmd(nc, in_maps, core_ids=list(range(8)))
    return np.stack([r["out"] for r in res.results], axis=0).astype(np.float32)
```
